# Optimizing a Trainium2 kernel written in Bass

```python
import math
import jax
import jax.numpy as jnp
from jax import lax
import numpy as np

D_MODEL = 1024
BATCH = 32
SEQ = 256
DEPTH = 2
DEC_BATCH = 4
DEC_SEQ = 4096
PAST_LEN = 512

GRID_W = 64
D_MIX = D_MODEL
GROUP_W = D_MIX // 4
ATT_HEADS = 4
ATT_KV_HEADS = 2
ATT_HEAD_DIM = GROUP_W // ATT_HEADS
WINDOW = 128
BLOCK = 128
DIF_HEADS = 4
DIF_V_DIM = GROUP_W // DIF_HEADS
DIF_QK_DIM = DIF_V_DIM // 2
HY_CH = GROUP_W
HY_STREAMS = 3
FILT_BANDS = 16
FILT_EMB = 1 + 2 * FILT_BANDS
FILT_HIDDEN = 64
HY_DECAY_TARGET = 1e-2
HY_FAST_DECAY = 0.3
HY_SLOW_DECAY = 1.5
HY_MIN_DECAY = math.log(HY_DECAY_TARGET) / HY_SLOW_DECAY
HY_MAX_DECAY = math.log(HY_DECAY_TARGET) / HY_FAST_DECAY
FN_GROUPS = 4
FN_GROUP_CH = GROUP_W // FN_GROUPS
ROPE_BASE = 10000.0
EPS = 1e-6
NEG_INF = -1e30

SPLIT_SIZES = (ATT_HEADS * ATT_HEAD_DIM, ATT_KV_HEADS * ATT_HEAD_DIM, ATT_KV_HEADS * ATT_HEAD_DIM, GROUP_W,
               2 * DIF_HEADS * DIF_QK_DIM, 2 * DIF_HEADS * DIF_QK_DIM, DIF_HEADS * DIF_V_DIM, GROUP_W,
               HY_STREAMS * HY_CH, GROUP_W,
               GROUP_W, GROUP_W)
D_IN = sum(SPLIT_SIZES)

kernel_name = 'hybrid_diffusion_parallel_heads_step'

F32 = jnp.float32


def rms_norm(x, g):
    xf = x.astype(F32)
    y = xf * lax.rsqrt(jnp.mean(xf * xf, axis=-1, keepdims=True) + EPS)
    return (y * g.astype(F32)).astype(x.dtype)


def axial_rope_tables(n_tokens, head_dim):
    pos = jnp.arange(n_tokens)
    row = (pos // GRID_W).astype(F32)
    col = (pos % GRID_W).astype(F32)
    n_freq = head_dim // 4
    inv = ROPE_BASE ** (-jnp.arange(n_freq, dtype=F32) / n_freq)
    ang = jnp.concatenate([row[:, None] * inv, col[:, None] * inv], axis=-1)
    return jnp.cos(ang), jnp.sin(ang)


def apply_rope(x, cos, sin):
    shape = (cos.shape[0],) + (1,) * (x.ndim - 3) + (cos.shape[1],)
    cs, sn = cos.reshape(shape), sin.reshape(shape)
    x1, x2 = jnp.split(x.astype(F32), 2, axis=-1)
    return jnp.concatenate([x1 * cs - x2 * sn, x1 * sn + x2 * cs], axis=-1).astype(x.dtype)


def split_projection(proj):
    points, acc = [], 0
    for s in SPLIT_SIZES[:-1]:
        acc += s
        points.append(acc)
    return jnp.split(proj, points, axis=-1)


def sink_softmax_values(s, vals, sink):
    sk = sink.astype(F32)[None, :, :, None, None]
    m = jnp.maximum(jnp.max(s, axis=-1, keepdims=True), sk)
    p = jnp.exp(s - m)
    denom = jnp.sum(p, axis=-1, keepdims=True) + jnp.exp(sk - m)
    return jnp.einsum('bkgqn,bnkd->bqkgd', p / denom, vals.astype(F32))


def gqa_context(q, k, v, sink):
    bsz, n = q.shape[:2]
    grp = ATT_HEADS // ATT_KV_HEADS
    qb = q.reshape(bsz, n // BLOCK, BLOCK, ATT_KV_HEADS, grp, ATT_HEAD_DIM).swapaxes(0, 1)
    sk = sink.reshape(ATT_KV_HEADS, grp)
    scale = ATT_HEAD_DIM ** -0.5

    def block(qblk):
        s = jnp.einsum('bqkgd,bnkd->bkgqn', qblk, k).astype(F32) * scale
        return sink_softmax_values(s, v, sk)

    o = lax.map(block, qb)
    return o.swapaxes(0, 1).reshape(bsz, n, ATT_HEADS * ATT_HEAD_DIM)


def gqa_latent(q, k, v, ctx_k, ctx_v, sink):
    bsz, n = q.shape[:2]
    grp = ATT_HEADS // ATT_KV_HEADS
    nb = n // BLOCK
    qb = q.reshape(bsz, nb, BLOCK, ATT_KV_HEADS, grp, ATT_HEAD_DIM).swapaxes(0, 1)
    sk = sink.reshape(ATT_KV_HEADS, grp)
    scale = ATT_HEAD_DIM ** -0.5
    pad = ((0, 0), (BLOCK, BLOCK), (0, 0), (0, 0))
    kp, vp = jnp.pad(k, pad), jnp.pad(v, pad)
    offs = jnp.arange(3 * BLOCK)
    qoffs = jnp.arange(BLOCK)

    def block(args):
        i, qblk = args
        start = i * BLOCK
        kw = lax.dynamic_slice_in_dim(kp, start, 3 * BLOCK, axis=1)
        vw = lax.dynamic_slice_in_dim(vp, start, 3 * BLOCK, axis=1)
        kpos = start - BLOCK + offs
        qpos = start + qoffs
        valid = ((jnp.abs(qpos[:, None] - kpos[None, :]) <= WINDOW)
                 & (kpos >= 0)[None, :] & (kpos < n)[None, :])
        s_win = jnp.einsum('bqkgd,bnkd->bkgqn', qblk, kw).astype(F32) * scale
        s_win = jnp.where(valid, s_win, NEG_INF)
        s_ctx = jnp.einsum('bqkgd,bnkd->bkgqn', qblk, ctx_k).astype(F32) * scale
        s = jnp.concatenate([s_win, s_ctx], axis=-1)
        vals = jnp.concatenate([vw, ctx_v.astype(vw.dtype)], axis=1)
        return sink_softmax_values(s, vals, sk)

    o = lax.map(block, (jnp.arange(nb), qb))
    return o.swapaxes(0, 1).reshape(bsz, n, ATT_HEADS * ATT_HEAD_DIM)


def diff_attention(q, keys, vals, lam, subln, lam_init):
    bsz, n = q.shape[:2]
    qb = q.reshape(bsz, n // BLOCK, BLOCK, 2, DIF_HEADS, DIF_QK_DIM).swapaxes(0, 1)
    scale = DIF_QK_DIM ** -0.5
    vf = vals.astype(F32)

    def block(qblk):
        s = jnp.einsum('bqmhd,bnmhd->bmhqn', qblk, keys).astype(F32) * scale
        p = jax.nn.softmax(s, axis=-1)
        a = p[:, 0] - lam * p[:, 1]
        return jnp.einsum('bhqn,bnhd->bqhd', a, vf)

    o = lax.map(block, qb).swapaxes(0, 1).reshape(bsz, n, DIF_HEADS, DIF_V_DIM)
    o = rms_norm(o, subln) * (1.0 - lam_init)
    return o.reshape(bsz, n, DIF_HEADS * DIF_V_DIM)


def hyena_filters(n, w1, b1, w2, b2, w3, freq):
    t = jnp.linspace(0.0, 1.0, n, dtype=F32)[:, None]
    w = (2.0 * math.pi / n) * jnp.arange(n, dtype=F32)[:, None]
    f = jnp.linspace(1e-4, FILT_BANDS - 1, FILT_BANDS, dtype=F32)[None, :]
    feats = jnp.concatenate([t, jnp.cos(f * w), -jnp.sin(f * w)], axis=-1)
    fr = freq.astype(F32)
    h = jnp.sin(fr * (feats @ w1.astype(F32) + b1.astype(F32)))
    h = jnp.sin(fr * (h @ w2.astype(F32) + b2.astype(F32)))
    h = (h @ w3.astype(F32)).reshape(n, 2, HY_CH)
    deltas = jnp.abs(jnp.linspace(HY_MIN_DECAY, HY_MAX_DECAY, HY_CH, dtype=F32))
    h = h * jnp.exp(-t * deltas)[:, None, :]
    h = h / (jnp.sum(jnp.abs(h), axis=(0, 1), keepdims=True) + EPS)
    return h[:, 0], h[:, 1]


def bidir_long_conv(z, hf, hb, skip):
    n = z.shape[1]
    kern = jnp.concatenate([hf, jnp.zeros((1, HY_CH), F32), hb[:0:-1]], axis=0)
    zf = z.astype(F32)
    y = jnp.fft.irfft(jnp.fft.rfft(zf, n=2 * n, axis=1) * jnp.fft.rfft(kern, axis=0)[None],
                      n=2 * n, axis=1)[:, :n]
    return y + zf * skip.astype(F32)


def hyena_mixer(u, conv_w, conv_b, w1, b1, w2, b2, w3, freq, skip):
    n = u.shape[1]
    up = jnp.pad(u, ((0, 0), (1, 1), (0, 0)))
    uc = up[:, :-2] * conv_w[0] + up[:, 1:-1] * conv_w[1] + up[:, 2:] * conv_w[2] + conv_b
    x0, x1, v = jnp.split(uc, 3, axis=-1)
    hf, hb = hyena_filters(n, w1, b1, w2, b2, w3, freq)
    return x0.astype(F32) * bidir_long_conv(x1 * v, hf, hb, skip)


def fnet_mixer(u, w, b):
    bsz, n, _ = u.shape
    ug = u.astype(F32).reshape(bsz, n, FN_GROUPS, FN_GROUP_CH)
    f = jnp.fft.fft2(ug, axes=(1, 3), norm='ortho').real
    return f.reshape(bsz, n, GROUP_W) @ w.astype(F32) + b.astype(F32)


def layer_entry(x, cond, w_ada_l, b_ada_l, g_pre_l, w_in_l):
    mod = jax.nn.silu(cond) @ w_ada_l + b_ada_l
    shift, scale, gate = jnp.split(mod[:, None, :], 3, axis=-1)
    h = rms_norm(x, g_pre_l) * (1.0 + scale) + shift
    return split_projection(h @ w_in_l), gate


def layer_exit(x, outs, gates, gate, w_out_l, g_post_l):
    mixed = jnp.concatenate([o.astype(x.dtype) * jax.nn.silu(g) for o, g in zip(outs, gates)], axis=-1)
    return x + gate * rms_norm(mixed @ w_out_l, g_post_l)


def setup_inputs(seed: int = 0) -> dict:
    key = jax.random.key(seed)
    ks = jax.random.split(key, 28)

    def nrm(k, shape, s):
        return s * jax.random.normal(k, shape, F32)

    return {
        'x_prompt': nrm(ks[0], (BATCH, SEQ, D_MODEL), 1.0),
        'x_sample': nrm(ks[1], (DEC_BATCH, DEC_SEQ, D_MODEL), 1.0),
        'cache_attn_k': nrm(ks[2], (DEC_BATCH, DEPTH, PAST_LEN, ATT_KV_HEADS, ATT_HEAD_DIM), 1.0),
        'cache_attn_v': nrm(ks[3], (DEC_BATCH, DEPTH, PAST_LEN, ATT_KV_HEADS, ATT_HEAD_DIM), 1.0),
        'cache_diff_k': nrm(ks[4], (DEC_BATCH, DEPTH, PAST_LEN, 2, DIF_HEADS, DIF_QK_DIM), 1.0),
        'cache_diff_v': nrm(ks[5], (DEC_BATCH, DEPTH, PAST_LEN, DIF_HEADS, DIF_V_DIM), 1.0),
        'c': nrm(ks[6], (DEC_BATCH, D_MODEL), 1.0),
        'c_ctx': nrm(ks[7], (D_MODEL,), 1.0),
        'w_ada': nrm(ks[8], (DEPTH, D_MODEL, 3 * D_MODEL), D_MODEL ** -0.5),
        'b_ada': nrm(ks[9], (DEPTH, 3 * D_MODEL), 0.01),
        'norm_pre': 1.0 + nrm(ks[10], (DEPTH, D_MODEL), 0.05),
        'norm_post': 1.0 + nrm(ks[11], (DEPTH, D_MODEL), 0.05),
        'w_in': nrm(ks[12], (DEPTH, D_MODEL, D_IN), D_MODEL ** -0.5),
        'w_out': nrm(ks[13], (DEPTH, D_MIX, D_MODEL), D_MIX ** -0.5),
        'attn_sink': nrm(ks[14], (DEPTH, ATT_HEADS), 0.5),
        'diff_lambda': nrm(ks[15], (DEPTH, 4, DIF_QK_DIM), 0.1),
        'diff_subln': 1.0 + nrm(ks[16], (DEPTH, DIF_V_DIM), 0.05),
        'hy_conv_w': nrm(ks[17], (DEPTH, 3, HY_STREAMS * HY_CH), 3.0 ** -0.5),
        'hy_conv_b': nrm(ks[18], (DEPTH, HY_STREAMS * HY_CH), 0.01),
        'hy_filt_w1': nrm(ks[19], (DEPTH, FILT_EMB, FILT_HIDDEN), FILT_EMB ** -0.5),
        'hy_filt_b1': nrm(ks[20], (DEPTH, FILT_HIDDEN), 0.1),
        'hy_filt_w2': nrm(ks[21], (DEPTH, FILT_HIDDEN, FILT_HIDDEN), FILT_HIDDEN ** -0.5),
        'hy_filt_b2': nrm(ks[22], (DEPTH, FILT_HIDDEN), 0.1),
        'hy_filt_w3': nrm(ks[23], (DEPTH, FILT_HIDDEN, 2 * HY_CH), FILT_HIDDEN ** -0.5),
        'hy_filt_freq': 1.0 + nrm(ks[24], (DEPTH, FILT_HIDDEN), 0.05),
        'hy_skip': nrm(ks[25], (DEPTH, HY_CH), 0.5),
        'fn_w': nrm(ks[26], (DEPTH, GROUP_W, GROUP_W), GROUP_W ** -0.5),
        'fn_b': nrm(ks[27], (DEPTH, GROUP_W), 0.01),
    }


def reference(x_prompt, x_sample, cache_attn_k, cache_attn_v, cache_diff_k, cache_diff_v, c, c_ctx,
              w_ada, b_ada, norm_pre, norm_post, w_in, w_out, attn_sink, diff_lambda, diff_subln,
              hy_conv_w, hy_conv_b, hy_filt_w1, hy_filt_b1, hy_filt_w2, hy_filt_b2, hy_filt_w3,
              hy_filt_freq, hy_skip, fn_w, fn_b):
    bp, lp, _ = x_prompt.shape
    bs, ls, _ = x_sample.shape
    cos_a, sin_a = axial_rope_tables(ls, ATT_HEAD_DIM)
    cos_d, sin_d = axial_rope_tables(ls, DIF_QK_DIM)
    xp, xs = x_prompt, x_sample
    st_ak, st_av, st_dk, st_dv = [], [], [], []
    for l in range(DEPTH):
        lam_init = 0.8 - 0.6 * math.exp(-0.3 * l)
        lam_par = diff_lambda[l].astype(F32)
        lam = jnp.exp(jnp.sum(lam_par[0] * lam_par[1])) - jnp.exp(jnp.sum(lam_par[2] * lam_par[3])) + lam_init
        hy_args = (hy_conv_w[l], hy_conv_b[l], hy_filt_w1[l], hy_filt_b1[l], hy_filt_w2[l], hy_filt_b2[l],
                   hy_filt_w3[l], hy_filt_freq[l], hy_skip[l])
        entry = (w_ada[l], b_ada[l], norm_pre[l], w_in[l])

        (aq, ak, av, ag, dq, dk, dv, dg, hu, hg, fu, fg), gate = layer_entry(xp, c_ctx[None, :], *entry)
        aq = aq.reshape(bp, lp, ATT_HEADS, ATT_HEAD_DIM)
        ak = ak.reshape(bp, lp, ATT_KV_HEADS, ATT_HEAD_DIM)
        av = av.reshape(bp, lp, ATT_KV_HEADS, ATT_HEAD_DIM)
        dq = dq.reshape(bp, lp, 2, DIF_HEADS, DIF_QK_DIM)
        dk = dk.reshape(bp, lp, 2, DIF_HEADS, DIF_QK_DIM)
        dv = dv.reshape(bp, lp, DIF_HEADS, DIF_V_DIM)
        outs = (gqa_context(aq, ak, av, attn_sink[l]),
                diff_attention(dq, dk, dv, lam, diff_subln[l], lam_init),
                hyena_mixer(hu, *hy_args),
                fnet_mixer(fu, fn_w[l], fn_b[l]))
        st_ak.append(ak)
        st_av.append(av)
        st_dk.append(dk)
        st_dv.append(dv)
        xp = layer_exit(xp, outs, (ag, dg, hg, fg), gate, w_out[l], norm_post[l])

        (aq, ak, av, ag, dq, dk, dv, dg, hu, hg, fu, fg), gate = layer_entry(xs, c, *entry)
        aq = apply_rope(aq.reshape(bs, ls, ATT_HEADS, ATT_HEAD_DIM), cos_a, sin_a)
        ak = apply_rope(ak.reshape(bs, ls, ATT_KV_HEADS, ATT_HEAD_DIM), cos_a, sin_a)
        av = av.reshape(bs, ls, ATT_KV_HEADS, ATT_HEAD_DIM)
        dq = apply_rope(dq.reshape(bs, ls, 2, DIF_HEADS, DIF_QK_DIM), cos_d, sin_d)
        dk = apply_rope(dk.reshape(bs, ls, 2, DIF_HEADS, DIF_QK_DIM), cos_d, sin_d)
        dv = dv.reshape(bs, ls, DIF_HEADS, DIF_V_DIM)
        keys = jnp.concatenate([dk, cache_diff_k[:, l].astype(dk.dtype)], axis=1)
        vals = jnp.concatenate([dv, cache_diff_v[:, l].astype(dv.dtype)], axis=1)
        outs = (gqa_latent(aq, ak, av, cache_attn_k[:, l], cache_attn_v[:, l], attn_sink[l]),
                diff_attention(dq, keys, vals, lam, diff_subln[l], lam_init),
                hyena_mixer(hu, *hy_args),
                fnet_mixer(fu, fn_w[l], fn_b[l]))
        xs = layer_exit(xs, outs, (ag, dg, hg, fg), gate, w_out[l], norm_post[l])

    return (xp, xs, jnp.stack(st_ak, axis=1), jnp.stack(st_av, axis=1), jnp.stack(st_dk, axis=1), jnp.stack(st_dv, axis=1))
```

```python
import math
import os
LVL = int(os.environ.get('ENTRY_LVL', '99'))
STQ = os.environ.get('STQ', 'pool')
SUB = os.environ.get('ENTRY_SUB', 'cde')
STS = os.environ.get('ENTRY_STS', '')
import numpy as np
import ml_dtypes
import concourse.bass as bass
import concourse.mybir as mybir
from concourse.bass_utils import run_bass_kernel_spmd

F32 = mybir.dt.float32
BF16 = mybir.dt.bfloat16
U8 = mybir.dt.uint8
AF = mybir.ActivationFunctionType
ALU = mybir.AluOpType
AX = mybir.AxisListType

NCORES = 8
D = 1024
DEPTH = 2
LS = 4096
LP = 256
NPR = 4
TOK = LS + NPR * LP
NT = TOK // 128
NST = TOK // 512
PAST = 512
EPS = 1e-6
DIN = 3328
NTM = 2304
NFM = 1024

C_AQ, C_AK, C_DQ, C_DK, C_AV, C_DV, C_AG, C_DG, C_FU, C_FG = 0, 256, 384, 640, 896, 1024, 1280, 1536, 1792, 2048


class Res:
    __slots__ = ("name", "w", "r", "excl")

    def __init__(self, name, excl=False):
        self.name = name
        self.excl = excl
        self.w = []
        self.r = []


class Op:
    __slots__ = ("eng", "fn", "deps", "dma", "dsem", "dval", "signal", "count", "pre")

    def __init__(self, eng, fn, dma):
        self.eng = eng
        self.fn = fn
        self.dma = dma
        self.deps = []
        self.dsem = None
        self.dval = 0
        self.signal = False
        self.count = 0
        self.pre = None


ENGS = ("pe", "act", "dve", "pool", "sp")


class Rec:
    def __init__(self, nc):
        self.nc = nc
        self.ops = {e: [] for e in ENGS}
        self.csem = {e: nc.alloc_semaphore("c_" + e) for e in ENGS}
        self.npool = {"sp": 40, "pool": 40, "act": 16}
        self.dpool = {q: [nc.alloc_semaphore("d_%s_%d" % (q, i)) for i in range(n)] for q, n in self.npool.items()}
        self.ndma = {q: 0 for q in self.npool}
        self.alldma = []

    def op(self, eng, fn, reads=(), writes=(), dma=False, accum=False):
        o = Op(eng, fn, dma)
        deps = []
        for r in reads:
            for d in r.w:
                deps.append(d)
            if r.excl:
                for d in r.r:
                    if d.dma or d.eng != eng:
                        deps.append(d)
        for w in writes:
            for d in w.w:
                if accum and (not d.dma) and (not dma) and d.eng == eng:
                    continue
                if accum and d.dma and dma:
                    continue
                deps.append(d)
            for d in w.r:
                if (not d.dma) and (not dma) and d.eng == eng:
                    continue
                deps.append(d)
        seen = set()
        for d in deps:
            if d is o or id(d) in seen:
                continue
            seen.add(id(d))
            if (not d.dma) and (not dma) and d.eng == eng and eng == "pe":
                continue
            o.deps.append(d)
        if dma:
            i = self.ndma[eng]
            self.ndma[eng] = i + 1
            P = self.npool[eng]
            o.dsem = self.dpool[eng][i % P]
            o.dval = 16 * (i // P + 1)
            if i >= P:
                o.pre = (o.dsem, 16 * (i // P))
            self.alldma.append(o)
        for r in reads:
            if not dma:
                r.r = [x for x in r.r if x.dma or x.eng != eng]
            r.r.append(o)
        for w in writes:
            if dma:
                if accum:
                    w.w = w.w + [o]
                else:
                    w.w = [o]
            else:
                if accum:
                    w.w = [x for x in w.w if x.dma or x.eng != eng] + [o]
                else:
                    w.w = [x for x in w.w if (not x.dma) and x.eng != eng] + [o]
            w.r = []
        self.ops[eng].append(o)
        return o

    def barrier(self):
        lasts = []
        for e in ENGS:
            for o in reversed(self.ops[e]):
                if not o.dma and o.fn is not None:
                    lasts.append(o)
                    break
        pend = list(self.alldma)
        self.alldma = []
        for e in ENGS:
            o = Op(e, None, False)
            o.deps = [d for d in lasts if d.eng != e] + pend
            self.ops[e].append(o)

    def emit(self):
        nc = self.nc
        for e in ENGS:
            for o in self.ops[e]:
                for d in o.deps:
                    if not d.dma:
                        d.signal = True
        for e in ENGS:
            c = 0
            for o in self.ops[e]:
                if o.signal and not o.dma:
                    c += 1
                    o.count = c
        final_waits = []
        for q in self.npool:
            n = self.ndma[q]
            P = self.npool[q]
            for j in range(min(n, P)):
                cnt = (n - 1 - j) // P + 1
                final_waits.append((self.dpool[q][j], 16 * cnt))

        def run(ename, eng):
            seen = {}
            for o in self.ops[ename]:
                waits = []
                if o.pre is not None:
                    waits.append(o.pre)
                for d in o.deps:
                    if d.dma:
                        waits.append((d.dsem, d.dval))
                    else:
                        waits.append((self.csem[d.eng], d.count))
                for (s, v) in waits:
                    k = id(s)
                    if seen.get(k, 0) >= v:
                        continue
                    seen[k] = v
                    eng.wait_ge(s, v)
                if o.fn is None:
                    continue
                ins = o.fn(eng)
                if o.dma:
                    ins.then_inc(o.dsem, 16)
                elif o.signal:
                    ins.then_inc(self.csem[ename], 1)
            if ename == "sp":
                for (s, v) in final_waits:
                    if seen.get(id(s), 0) < v:
                        eng.wait_ge(s, v)

        with nc.Block() as blk:
            @blk.tensor
            def _(e):
                run("pe", e)

            @blk.scalar
            def _(e):
                run("act", e)

            @blk.vector
            def _(e):
                run("dve", e)

            @blk.gpsimd
            def _(e):
                run("pool", e)

            @blk.sync
            def _(e):
                run("sp", e)


class Arena:
    def __init__(self, nc, nbytes):
        self.t = nc.alloc_sbuf_tensor("arena", [128, nbytes], U8)
        self.nbytes = nbytes
        self.top = 0
        self.marks = []

    def alloc(self, shape, dtype):
        esz = 2 if dtype == BF16 else 4
        n = 1
        for s in shape[1:]:
            n *= s
        nb = (n * esz + 63) // 64 * 64
        assert self.top + nb <= self.nbytes, ("arena overflow", self.top, nb)
        ap = self.t[:, self.top:self.top + n * esz].bitcast(dtype)
        self.top += nb
        if len(shape) == 3:
            ap = ap.rearrange("p (a b) -> p a b", b=shape[2])
        elif len(shape) == 4:
            ap = ap.rearrange("p (a b c) -> p a b c", b=shape[2], c=shape[3])
        if shape[0] < 128:
            ap = ap[0:shape[0]]
        return ap

    def mark(self):
        self.marks.append(self.top)

    def release(self):
        self.top = self.marks.pop()


def _rope_tables(n, head_dim):
    pos = np.arange(n)
    row = (pos // 64).astype(np.float32)
    col = (pos % 64).astype(np.float32)
    nf = head_dim // 4
    inv = (10000.0 ** (-np.arange(nf, dtype=np.float32) / nf)).astype(np.float32)
    ang = np.concatenate([row[:, None] * inv, col[:, None] * inv], axis=-1).astype(np.float32)
    return np.cos(ang).astype(np.float32), np.sin(ang).astype(np.float32)


def _consts():
    c = {}
    ca, sa = _rope_tables(LS, 64)
    cd, sd = _rope_tables(LS, 32)
    c["rope"] = np.concatenate([ca, sa, cd, sd], axis=1).astype(np.float32)
    sel = np.zeros((2, 256), np.float32)
    sel[0, :128] = 1.0
    sel[1, 128:] = 1.0
    c["sel"] = sel
    c["ident"] = np.eye(128, dtype=np.float32).astype(ml_dtypes.bfloat16)
    c["identf"] = np.eye(128, dtype=np.float32)
    j = np.arange(128)
    c["mprev"] = (j[:, None] >= j[None, :]).astype(np.float32).astype(ml_dtypes.bfloat16)
    c["mnext"] = (j[:, None] <= j[None, :]).astype(np.float32).astype(ml_dtypes.bfloat16)
    jj = np.arange(64)
    blk = 2.0 * np.pi * jj[:, None] * jj[None, :] / 64.0
    bc = np.zeros((128, 128), np.float32)
    bs = np.zeros((128, 128), np.float32)
    for g in range(2):
        bc[g * 64:(g + 1) * 64, g * 64:(g + 1) * 64] = np.cos(blk)
        bs[g * 64:(g + 1) * 64, g * 64:(g + 1) * 64] = -np.sin(blk)
    c["bdc"] = bc
    c["bds"] = bs
    c["ones"] = np.ones((128, 128), np.float32)
    for n, tag in ((LS, "s"), (LP, "p")):
        nt = n // 128
        N = 2 * n
        t = np.arange(n, dtype=np.float64)
        KW = 256
        th = 2.0 * np.pi * np.outer(t, t) / n
        for nm, fn in (("fc", np.cos), ("fs", np.sin)):
            m = fn(th).astype(np.float32)
            for rv in (0, 1):
                mm_ = m[::-1, ::-1] if rv else m
                mm_ = mm_.reshape(nt, 128, n // KW, KW).transpose(2, 1, 0, 3)
                c[nm + tag + ("r" if rv else "")] = np.ascontiguousarray(mm_).astype(ml_dtypes.bfloat16)
        th = 2.0 * np.pi * np.outer(t + 0.5, t + 0.5) / N
        for nm, fn in (("hc", np.cos), ("hs", np.sin)):
            m = fn(th).astype(np.float32)
            m = m.reshape(nt, 128, nt, 128).transpose(2, 1, 0, 3)
            c[nm + tag] = np.ascontiguousarray(m).astype(ml_dtypes.bfloat16)
        ang = 2.0 * np.pi * (t + 0.5) / N / 2.0
        rot = np.stack([np.cos(ang), np.sin(ang), -np.cos(ang)], axis=-1).astype(np.float32)
        c["rot" + tag] = np.ascontiguousarray(rot.reshape(nt, 128, 3).transpose(1, 0, 2))
        tl = np.linspace(0.0, 1.0, n, dtype=np.float32)[:, None]
        w = (2.0 * math.pi / n) * np.arange(n, dtype=np.float32)[:, None]
        f = np.linspace(1e-4, 15, 16, dtype=np.float32)[None, :]
        feats = np.concatenate([tl, np.cos(f * w), -np.sin(f * w)], axis=-1).astype(np.float32)
        c["feat" + tag] = np.ascontiguousarray(feats.T)
        hmin = math.log(1e-2) / 1.5
        hmax = math.log(1e-2) / 0.3
        deltas = np.abs(np.linspace(hmin, hmax, 256, dtype=np.float32))
        c["dec" + tag] = np.exp(-tl * deltas).astype(np.float32)
    return c


WIN_PERM = None


def _win_perm():
    aq = np.arange(0, 256).reshape(4, 64)[[0, 2, 1, 3]].reshape(-1)
    ak = np.arange(256, 384)
    av = np.arange(384, 512)
    ag = np.arange(512, 768)
    dq = np.arange(768, 1024)
    dk = np.arange(1024, 1280)
    dv = np.arange(1280, 1536)
    dg = np.arange(1536, 1792)
    hu = np.arange(1792, 2560)
    hg = np.arange(2560, 2816)
    fu = np.arange(2816, 3072)
    fg = np.arange(3072, 3328)
    return np.concatenate([aq, ak, dq, dk, av, dv, ag, dg, fu, fg, hu, hg])


class Builder:
    def __init__(self, debug=None, nlayers=DEPTH, phases=None):
        self.debug = debug or ()
        self.nlayers = nlayers
        self.phases = phases or ("setup", "entry", "mix", "exit")
        nc = bass.Bass("TRN2", target_bir_lowering=False)
        self.nc = nc
        self.R = Rec(nc)
        self.A = Arena(nc, 200 * 1024)
        self.psum = nc.alloc_psum_tensor("ps", [128, 8 * 512], F32)
        self.pres = [Res("ps%d" % i, excl=True) for i in range(8)]
        self.inp = {}
        self.out = {}

        def din(name, shape, dt=F32):
            self.inp[name] = nc.dram_tensor(name, list(shape), dt, kind="ExternalInput").ap()

        def dout(name, shape, dt=F32):
            self.out[name] = nc.dram_tensor(name, list(shape), dt, kind="ExternalOutput").ap()

        def dscr(name, shape, dt):
            return nc.dram_tensor(name, list(shape), dt, kind="Internal").ap()

        din("xs", [LS, D])
        din("xp", [NPR * LP, D])
        din("condT", [128, 8, 2])
        din("cak", [DEPTH, PAST, 128])
        din("cav", [DEPTH, PAST, 128])
        din("cdk", [DEPTH, PAST, 256])
        din("cdv", [DEPTH, PAST, 256])
        din("w_ada", [DEPTH, D, 3 * D])
        din("b_ada", [DEPTH, 3 * D])
        din("norm_pre", [DEPTH, D])
        din("norm_post", [DEPTH, D])
        din("w_in", [DEPTH, D, DIN])
        din("w_out", [DEPTH, D, D])
        din("rope", [LS, 96])
        din("sel", [2, 256])
        din("ident", [128, 128], BF16)
        din("identf", [128, 128])
        din("mprev", [128, 128], BF16)
        din("mnext", [128, 128], BF16)
        din("bdc", [128, 128])
        din("bds", [128, 128])
        din("ones", [128, 128])
        for n, tag in ((LS, "s"), (LP, "p")):
            nt = n // 128
            din("fc" + tag, [n // 256, 128, nt, 256], BF16)
            din("fs" + tag, [n // 256, 128, nt, 256], BF16)
            din("hc" + tag, [nt, 128, nt, 128], BF16)
            din("hs" + tag, [nt, 128, nt, 128], BF16)
            din("rot" + tag, [128, nt, 3])
            din("feat" + tag, [33, n])
            din("dec" + tag, [n, 256])
        din("attn_sink", [DEPTH, 4])
        din("diff_lambda", [DEPTH, 128])
        din("diff_subln", [DEPTH, 64])
        din("hcw", [DEPTH, 128, 6, 4])
        din("hskip", [DEPTH, 128, 2])
        din("hw1", [DEPTH, 33, 64])
        din("hb1", [DEPTH, 64, 1])
        din("hw2", [DEPTH, 64, 64])
        din("hb2", [DEPTH, 64, 1])
        din("hw3", [DEPTH, 64, 512])
        din("hfreq", [DEPTH, 64, 1])
        din("fn_w", [DEPTH, 256, 256])
        din("fn_b", [DEPTH, 256])
        din("hflag", [1, 2])
        dout("ys", [LS // 2, D])
        dout("yp", [NPR * LP, D])
        dout("nak", [NPR, DEPTH, LP, 128])
        dout("nav", [NPR, DEPTH, LP, 128])
        dout("ndk", [NPR, DEPTH, LP, 256])
        dout("ndv", [NPR, DEPTH, LP, 256])
        S = {}
        S["xcur"] = dscr("xcur", [TOK, D], F32)
        S["QKT"] = dscr("QKT", [896, TOK], BF16)
        S["VS"] = dscr("VS", [TOK, 384], BF16)
        S["SG"] = dscr("SG", [TOK, 768], BF16)
        S["FU"] = dscr("FU", [TOK, 256], BF16)
        S["HUT"] = dscr("HUT", [768, TOK], BF16)
        S["SGHT"] = dscr("SGHT", [256, TOK], BF16)
        S["MIX"] = dscr("MIX", [TOK, D], BF16)
        S["HB"] = dscr("HB", [LS, 256], F32)
        self.S = S
        for nm in self.debug:
            src = S[nm]
            dout("dbg_" + nm, list(src.shape), BF16 if src.dtype == BF16 else F32)

    def bank(self, i, dt=F32):
        ap = self.psum[:, i * 512:(i + 1) * 512]
        if dt == BF16:
            ap = ap.bitcast(BF16)
        return ap

    def dma(self, q, out, in_, reads=(), writes=(), accum=False):
        return self.R.op(q, lambda e, out=out, in_=in_: e.dma_start(out=out, in_=in_), reads=reads, writes=writes, dma=True, accum=accum)

    def memset(self, eng, ap, val, writes):
        return self.R.op(eng, lambda e, ap=ap, val=val: e.memset(ap, val), writes=writes)

    def recip(self, out, in_, reads, writes):
        return self.R.op("dve", lambda e, out=out, in_=in_: e.reciprocal(out=out, in_=in_), reads=reads, writes=writes)

    def mm(self, out, lhsT, rhs, start, stop, reads, writes, tile_position=None):
        kw = {}
        if tile_position is not None:
            kw["tile_position"] = tile_position
        return self.R.op("pe", lambda e: e.matmul(out, lhsT, rhs, start=start, stop=stop, **kw),
                         reads=reads, writes=writes, accum=not start)

    def tr(self, out, in_, reads, writes, ident=None):
        ident = self.ident if ident is None else ident
        p = in_.shape[0]
        return self.R.op("pe", lambda e: e.transpose(out, in_, ident[0:p, 0:p]), reads=list(reads) + [self.r_ident], writes=writes, accum=True)

    def act(self, out, in_, func, reads, writes, **kw):
        return self.R.op("act", lambda e: e.activation(out=out, in_=in_, func=func, **kw), reads=reads, writes=writes)

    def tt(self, eng, out, in0, in1, op, reads, writes):
        return self.R.op(eng, lambda e: e.tensor_tensor(out=out, in0=in0, in1=in1, op=op), reads=reads, writes=writes)

    def ts(self, eng, out, in0, s1, s2, op0, op1, reads, writes):
        return self.R.op(eng, lambda e: e.tensor_scalar(out=out, in0=in0, scalar1=s1, scalar2=s2, op0=op0, op1=op1), reads=reads, writes=writes)

    def stt(self, out, in0, scalar, in1, op0, op1, reads, writes):
        return self.R.op("dve", lambda e: e.scalar_tensor_tensor(out=out, in0=in0, scalar=scalar, in1=in1, op0=op0, op1=op1), reads=reads, writes=writes)

    def cp(self, eng, out, in_, reads, writes):
        if eng == "act":
            return self.R.op("act", lambda e: e.activation(out=out, in_=in_, func=AF.Copy), reads=reads, writes=writes)
        return self.R.op(eng, lambda e: e.tensor_copy(out=out, in_=in_), reads=reads, writes=writes)

    def build(self):
        A, R = self.A, self.R
        I, S = self.inp, self.S
        self.ident = A.alloc([128, 128], BF16)
        self.r_ident = Res("ident")
        self.dma("sp", self.ident, I["ident"], writes=[self.r_ident])
        self.sel = A.alloc([2, 256], F32)
        self.r_sel = Res("sel")
        self.dma("sp", self.sel, I["sel"], writes=[self.r_sel])
        self.neghalf = A.alloc([128, 1], F32)
        self.r_nh = Res("nh")
        R.op("dve", lambda e: e.memset(self.neghalf, -0.5), writes=[self.r_nh])
        condT = A.alloc([128, 8, 2], F32)
        self.scond = A.alloc([128, 8, 2], BF16)
        r_c = Res("condT")
        self.r_scond = Res("scond")
        self.dma("sp", condT, I["condT"], writes=[r_c])
        self.act(self.scond, condT, AF.Silu, [r_c], [self.r_scond])
        self.GG = [A.alloc([128, D], F32) for _ in range(2)]
        self.r_mod = Res("mod")
        for l in range(self.nlayers):
            A.mark()
            self.SH = [A.alloc([128, D], F32) for _ in range(2)]
            self.GS = [A.alloc([128, D], F32) for _ in range(2)]
            if "setup" in self.phases:
                self.layer_setup(l)
                R.barrier()
            if "entry" in self.phases:
                self.entry(l)
                R.barrier()
            A.release()
            if "A" in self.phases or "mix" in self.phases:
                self.attnA(l)
                R.barrier()
            if "B" in self.phases or "mix" in self.phases:
                self.attnB(l)
                R.barrier()
            if "F" in self.phases or "mix" in self.phases:
                self.fnet(l)
                R.barrier()
            if "H" in self.phases or "mix" in self.phases:
                self.hyena(l)
                R.barrier()
            if "exit" in self.phases:
                self.exit(l)
                R.barrier()
        for nm in self.debug:
            self.dma("sp", self.out["dbg_" + nm], S[nm])
        R.emit()
        return self.nc

    def layer_setup(self, l):
        A, R, I = self.A, self.R, self.inp
        A.mark()
        wada = A.alloc([128, 8, 3 * D], BF16)
        r_wada = Res("wada")
        src = I["w_ada"][l].rearrange("(k p) n -> p k n", p=128)
        for k in range(8):
            for h in range(2):
                self.R.op("pool", lambda e, k=k, h=h: e.dma_start(out=wada[:, k, h * 1536:(h + 1) * 1536], in_=src[:, k, h * 1536:(h + 1) * 1536]),
                          writes=[r_wada], dma=True, accum=True)
        bada = A.alloc([2, 3 * D], F32)
        r_b = Res("bada")
        self.dma("sp", bada, I["b_ada"][l:l + 1, :].broadcast_to([2, 3 * D]), writes=[r_b])
        gpre = A.alloc([128, D], F32)
        gpost = A.alloc([128, D], F32)
        r_g = Res("gprepost")
        self.dma("sp", gpre, I["norm_pre"][l:l + 1, :].broadcast_to([128, D]), writes=[r_g])
        R.op("sp", lambda e: e.dma_start(out=gpost, in_=I["norm_post"][l:l + 1, :].broadcast_to([128, D])), writes=[r_g], dma=True, accum=True)
        modrow = A.alloc([2, 3 * D], F32)
        r_mr = Res("modrow")
        for c in range(6):
            pb = self.bank(c)[0:2, :]
            for k in range(8):
                self.mm(pb, self.scond[:, k, :], wada[:, k, c * 512:(c + 1) * 512], k == 0, k == 7,
                        [self.r_scond, r_wada], [self.pres[c]])
            self.tt("dve", modrow[:, c * 512:(c + 1) * 512], pb, bada[:, c * 512:(c + 1) * 512], ALU.add,
                    [self.pres[c], r_b], [r_mr])
        R.barrier()
        i = 0
        for cond in range(2):
            lhsT = self.sel[:, cond * 128:(cond + 1) * 128]
            for part in range(3):
                for h in range(2):
                    b = i % 8
                    i += 1
                    pb = self.bank(b)
                    self.mm(pb, lhsT, modrow[:, part * D + h * 512: part * D + (h + 1) * 512], True, True,
                            [self.r_sel, r_mr], [self.pres[b]])
                    sl = slice(h * 512, (h + 1) * 512)
                    if part == 0:
                        self.cp("dve", self.SH[cond][:, sl], pb, [self.pres[b]], [self.r_mod])
                    elif part == 1:
                        self.stt(self.GS[cond][:, sl], pb, 1.0, gpre[:, sl], ALU.add, ALU.mult, [self.pres[b], r_g], [self.r_mod])
                    else:
                        self.tt("dve", self.GG[cond][:, sl], pb, gpost[:, sl], ALU.mult, [self.pres[b], r_g], [self.r_mod])
        A.release()

    def xsrc(self, l, T):
        if l == 0:
            if T < 32:
                return self.inp["xs"][T * 128:(T + 1) * 128, :]
            return self.inp["xp"][(T - 32) * 128:(T - 31) * 128, :]
        return self.S["xcur"][T * 128:(T + 1) * 128, :]

    def entry(self, l):
        A, R, I, S = self.A, self.R, self.inp, self.S
        A.mark()
        win = A.alloc([128, 8, DIN], BF16)
        r_win = Res("win")
        src = I["w_in"][l].rearrange("(k p) n -> p k n", p=128)
        for k in range(8):
            for h in range(2):
                R.op("pool", lambda e, k=k, h=h: e.dma_start(out=win[:, k, h * 1664:(h + 1) * 1664], in_=src[:, k, h * 1664:(h + 1) * 1664]),
                     writes=[r_win], dma=True, accum=True)
        NB = 2
        xt = [A.alloc([128, D], F32) for _ in range(NB)]
        r_xt = [Res("xt%d" % i) for i in range(NB)]
        junk = A.alloc([128, D], BF16)
        r_junk = Res("junk")
        ss = [A.alloc([128, 1], F32) for _ in range(NB)]
        rstd = [A.alloc([128, 1], F32) for _ in range(NB)]
        r_ss = [Res("ss%d" % i) for i in range(NB)]
        tmp = A.alloc([128, D], F32)
        r_tmp = Res("tmp")
        hb = [A.alloc([128, D], BF16) for _ in range(NB)]
        r_hb = [Res("hb%d" % i) for i in range(NB)]
        hT = [A.alloc([128, 8, 512], BF16) for _ in range(2)]
        r_hT = [Res("hT%d" % i) for i in range(2)]
        rope = [A.alloc([128, 96], F32) for _ in range(NB)]
        r_rope = [Res("rope%d" % i) for i in range(NB)]
        rq = [A.alloc([128, 896], BF16) for _ in range(NB)]
        r_rq = [Res("rq%d" % i) for i in range(NB)]
        rt = [A.alloc([128, 256], F32) for _ in range(4)]
        r_rt = [Res("rt%d" % i) for i in range(4)]
        qkt_st = [A.alloc([128, 7, 512], BF16) for _ in range(2)]
        vs_st = [A.alloc([128, 4, 384], BF16) for _ in range(2)]
        sg_st = [A.alloc([128, 4, 768], BF16) for _ in range(2)]
        fu_st = [A.alloc([128, 4, 256], BF16) for _ in range(2)]
        hut_st = [A.alloc([128, 6, 512], BF16) for _ in range(2)]
        sght_st = [A.alloc([128, 2, 512], BF16) for _ in range(2)]
        r_qkt = [Res("qkt%d" % i) for i in range(2)]
        r_vs = [Res("vs%d" % i) for i in range(2)]
        r_sg = [Res("sg%d" % i) for i in range(2)]
        r_fu = [Res("fu%d" % i) for i in range(2)]
        r_hut = [Res("hut%d" % i) for i in range(2)]
        r_sght = [Res("sght%d" % i) for i in range(2)]
        cst = [A.alloc([128, 768], F32) for _ in range(2)]
        r_cst = [Res("cst%d" % i) for i in range(2)]
        pres = self.pres
        QKTv = S["QKT"].rearrange("(c p) t -> p c t", p=128)
        HUTv = S["HUT"].rearrange("(c p) t -> p c t", p=128)
        SGHTv = S["SGHT"].rearrange("(c p) t -> p c t", p=128)
        ps = self.psum

        def stageA(T):
            st, tt_ = T // 4, T % 4
            cond = 0 if st < 8 else 1
            sb = st % 2
            b = T % NB
            self.dma("sp", xt[b], self.xsrc(l, T), writes=[r_xt[b]])
            if cond == 0:
                self.dma("sp", rope[b], I["rope"][T * 128:(T + 1) * 128, :], writes=[r_rope[b]])
            R.op("act", lambda e, b=b: e.activation(out=junk, in_=xt[b], func=AF.Square, accum_out=ss[b]),
                 reads=[r_xt[b]], writes=[r_junk, r_ss[b]])
            self.ts("dve", ss[b], ss[b], 1.0 / D, EPS, ALU.mult, ALU.add, [r_ss[b]], [r_ss[b]])
            self.tt("pool", rstd[b], ss[b], self.neghalf, ALU.pow, [r_ss[b], self.r_nh], [r_ss[b]])
            self.stt(tmp, xt[b], rstd[b], self.GS[cond], ALU.mult, ALU.mult, [r_xt[b], r_ss[b], self.r_mod], [r_tmp])
            self.tt("dve", hb[b], tmp, self.SH[cond], ALU.add, [r_tmp, self.r_mod], [r_hb[b]])
            pb = self.bank(5, BF16)
            for k in range(8):
                self.tr(pb[:, k * 128:(k + 1) * 128], hb[b][:, k * 128:(k + 1) * 128], [r_hb[b]], [pres[5]])
            self.cp("act", hT[sb][:, :, tt_ * 128:(tt_ + 1) * 128], pb.rearrange("p (k t) -> p k t", t=128),
                    [pres[5]], [r_hT[sb]])

        def stageB(T):
            st, tt_ = T // 4, T % 4
            sb = st % 2
            for c in range(5):
                w = 512 if c < 4 else 256
                for k in range(8):
                    self.mm(self.bank(c)[:, 0:w], hT[sb][:, k, tt_ * 128:(tt_ + 1) * 128], win[:, k, c * 512:c * 512 + w],
                            k == 0, k == 7, [r_hT[sb], r_win], [pres[c]])

        def stageC(T):
            st, tt_ = T // 4, T % 4
            cond = 0 if st < 8 else 1
            sb = st % 2
            b = T % NB
            if cond == 0:
                cosA = rope[b][:, 0:32].unsqueeze(1).broadcast_to([128, 6, 32])
                sinA = rope[b][:, 32:64].unsqueeze(1).broadcast_to([128, 6, 32])
                cosD = rope[b][:, 64:80].unsqueeze(1).broadcast_to([128, 16, 16])
                sinD = rope[b][:, 80:96].unsqueeze(1).broadcast_to([128, 16, 16])
                for (c0, nh, hd, cs, sn, banks) in ((0, 6, 64, cosA, sinA, [0]), (384, 16, 32, cosD, sinD, [0, 1])):
                    half = hd // 2
                    src3 = ps[:, c0:c0 + nh * hd].rearrange("p (h d) -> p h d", d=hd)
                    dst3 = rq[b][:, c0:c0 + nh * hd].rearrange("p (h d) -> p h d", d=hd)
                    x1 = src3[:, :, 0:half]
                    x2 = src3[:, :, half:hd]
                    n = nh * half
                    t = [rt[i][:, 0:n].rearrange("p (h d) -> p h d", d=half) for i in range(4)]
                    rb = [pres[i] for i in banks]
                    self.tt("dve", t[0], x1, cs, ALU.mult, rb + [r_rope[b]], [r_rt[0]])
                    self.tt("dve", t[1], x2, sn, ALU.mult, rb + [r_rope[b]], [r_rt[1]])
                    self.tt("dve", t[2], x1, sn, ALU.mult, rb + [r_rope[b]], [r_rt[2]])
                    self.tt("dve", t[3], x2, cs, ALU.mult, rb + [r_rope[b]], [r_rt[3]])
                    self.tt("pool", dst3[:, :, 0:half], t[0], t[1], ALU.subtract, [r_rt[0], r_rt[1]], [r_rq[b]])
                    self.tt("pool", dst3[:, :, half:hd], t[2], t[3], ALU.add, [r_rt[2], r_rt[3]], [r_rq[b]])
            else:
                self.cp("act", rq[b][:, 0:512], ps[:, 0:512], [pres[0]], [r_rq[b]])
                self.cp("act", rq[b][:, 512:896], ps[:, 512:896], [pres[1]], [r_rq[b]])
                cb = T % 2
                self.cp("dve", cst[cb][:, 0:128], ps[:, C_AK:C_AK + 128], [pres[0]], [r_cst[cb]])
                self.cp("dve", cst[cb][:, 128:384], ps[:, C_DK:C_DK + 256], [pres[1]], [r_cst[cb]])
                self.cp("dve", cst[cb][:, 384:768], ps[:, C_AV:C_AV + 384], [pres[1], pres[2]], [r_cst[cb]])
                pj = (T - 32) // 2
                t0 = ((T - 32) % 2) * 128
                self.dma(STQ, self.out["nak"][pj, l, t0:t0 + 128, :], cst[cb][:, 0:128], reads=[r_cst[cb]])
                self.dma(STQ, self.out["ndk"][pj, l, t0:t0 + 128, :], cst[cb][:, 128:384], reads=[r_cst[cb]])
                self.dma(STQ, self.out["nav"][pj, l, t0:t0 + 128, :], cst[cb][:, 384:512], reads=[r_cst[cb]])
                self.dma(STQ, self.out["ndv"][pj, l, t0:t0 + 128, :], cst[cb][:, 512:768], reads=[r_cst[cb]])
            self.cp("act", vs_st[sb][:, tt_, 0:128], ps[:, C_AV:C_AV + 128], [pres[1]], [r_vs[sb]])
            self.cp("act", vs_st[sb][:, tt_, 128:384], ps[:, C_DV:C_DV + 256], [pres[2]], [r_vs[sb]])
            self.act(sg_st[sb][:, tt_, 0:256], ps[:, C_AG:C_AG + 256], AF.Silu, [pres[2]], [r_sg[sb]])
            self.act(sg_st[sb][:, tt_, 256:512], ps[:, C_DG:C_DG + 256], AF.Silu, [pres[3]], [r_sg[sb]])
            self.act(sg_st[sb][:, tt_, 512:768], ps[:, C_FG:C_FG + 256], AF.Silu, [pres[4]], [r_sg[sb]])
            self.cp("act", fu_st[sb][:, tt_, :], ps[:, C_FU:C_FU + 256], [pres[3]], [r_fu[sb]])
            pq = self.bank(6, BF16)
            for c in range(7):
                self.tr(pq[:, c * 128:(c + 1) * 128], rq[b][:, c * 128:(c + 1) * 128], [r_rq[b]], [pres[6]])
            self.cp("dve", qkt_st[sb][:, :, tt_ * 128:(tt_ + 1) * 128], pq[:, 0:896].rearrange("p (c t) -> p c t", t=128),
                    [pres[6]], [r_qkt[sb]])

        def superFM(st):
            sb = st % 2
            for j in range(8):
                bk = 7
                for k in range(8):
                    self.mm(self.bank(bk), win[:, k, NTM + j * 128: NTM + (j + 1) * 128], hT[sb][:, k, :], k == 0, k == 7,
                            [r_win, r_hT[sb]], [pres[bk]])
                if j < 6:
                    self.cp("act" if j % 2 == 0 else "dve", hut_st[sb][:, j, :], self.bank(bk), [pres[bk]], [r_hut[sb]])
                else:
                    self.act(sght_st[sb][:, j - 6, :], self.bank(bk), AF.Silu, [pres[bk]], [r_sght[sb]])
            t0 = st * 512
            self.dma(STQ, QKTv[:, :, t0:t0 + 512], qkt_st[sb], reads=[r_qkt[sb]])
            self.dma(STQ, S["VS"][t0:t0 + 512, :].rearrange("(a p) c -> p a c", p=128), vs_st[sb], reads=[r_vs[sb]])
            self.dma(STQ, S["SG"][t0:t0 + 512, :].rearrange("(a p) c -> p a c", p=128), sg_st[sb], reads=[r_sg[sb]])
            self.dma(STQ, S["FU"][t0:t0 + 512, :].rearrange("(a p) c -> p a c", p=128), fu_st[sb], reads=[r_fu[sb]])
            self.dma(STQ, HUTv[:, :, t0:t0 + 512], hut_st[sb], reads=[r_hut[sb]])
            self.dma(STQ, SGHTv[:, :, t0:t0 + 512], sght_st[sb], reads=[r_sght[sb]])

        stageA(0)
        stageB(0)
        for T in range(NT):
            if T + 1 < NT:
                stageA(T + 1)
            stageC(T)
            if T % 4 == 3:
                superFM(T // 4)
            if T + 1 < NT:
                stageB(T + 1)
        A.release()

    def half(self, l):
        return self.nlayers == DEPTH and l == DEPTH - 1

    def seqs(self):
        return [("s", 0, 32)] + [("p%d" % j, 32 + 2 * j, 2) for j in range(NPR)]

    def load_ctx_T(self, src, ncol, dstT, r_dst, ps_bank):
        A = self.A
        A.mark()
        raw = A.alloc([128, 4, ncol], F32)
        rawb = A.alloc([128, 4, ncol], BF16)
        r_raw = Res("ctxraw")
        r_rawb = Res("ctxrawb")
        self.dma("sp", raw, src.rearrange("(a p) c -> p a c", p=128), writes=[r_raw])
        self.cp("dve", rawb, raw, [r_raw], [r_rawb])
        pb = self.bank(ps_bank, BF16)
        for c in range(ncol // 128):
            for a in range(4):
                self.tr(pb[:, a * 128:(a + 1) * 128], rawb[:, a, c * 128:(c + 1) * 128], [r_rawb], [self.pres[ps_bank]])
            if ncol == 128:
                self.cp("dve", dstT, pb[:, 0:512], [self.pres[ps_bank]], [r_dst])
            else:
                self.cp("dve", dstT[:, c, :], pb[:, 0:512], [self.pres[ps_bank]], [r_dst])
        A.release()
        return raw

    def attnA(self, l):
        A, R, I, S = self.A, self.R, self.inp, self.S
        pres = self.pres
        A.mark()
        QT = A.alloc([128, 2, TOK], BF16)
        KT = A.alloc([128, TOK], BF16)
        KTc = A.alloc([128, 512], BF16)
        Vaug = A.alloc([128, NT + 4, 2, 65], BF16)
        r_q = Res("aQT"); r_k = Res("aKT"); r_kc = Res("aKTc"); r_v = Res("aV")
        QKTv = S["QKT"].rearrange("(c p) t -> p c t", p=128)
        self.dma("sp", QT, QKTv[:, 0:2, :], writes=[r_q])
        self.dma("sp", KT, S["QKT"][256:384, :], writes=[r_k])
        mprev = A.alloc([128, 128], BF16); mnext = A.alloc([128, 128], BF16)
        r_m = Res("amask")
        self.dma("sp", mprev, I["mprev"], writes=[r_m])
        R.op("sp", lambda e: e.dma_start(out=mnext, in_=I["mnext"]), writes=[r_m], dma=True, accum=True)
        esink = A.alloc([128, 4], F32)
        r_es = Res("esink")
        self.dma("sp", esink, I["attn_sink"][l:l + 1, :].broadcast_to([128, 4]), writes=[r_es])
        self.act(esink, esink, AF.Exp, [r_es], [r_es])
        A.mark()
        vraw = A.alloc([128, NT, 128], BF16)
        vcraw = A.alloc([128, 4, 128], F32)
        r_vr = Res("avraw")
        self.dma("sp", vraw, S["VS"][:, 0:128].rearrange("(a p) c -> p a c", p=128), writes=[r_vr])
        R.op("sp", lambda e: e.dma_start(out=vcraw, in_=I["cav"][l].rearrange("(a p) c -> p a c", p=128)), writes=[r_vr], dma=True, accum=True)
        R.op("pool", lambda e: e.memset(Vaug, 1.0), writes=[r_v])
        self.cp("dve", Vaug[:, 0:NT, :, 0:64], vraw.rearrange("p a (g d) -> p a g d", d=64), [r_vr], [r_v])
        self.cp("dve", Vaug[:, NT:NT + 4, :, 0:64], vcraw.rearrange("p a (g d) -> p a g d", d=64), [r_vr], [r_v])
        self.load_ctx_T(I["cak"][l], 128, KTc, r_kc, 7)
        R.barrier()
        A.release()
        NB = 2
        gate = [A.alloc([128, 256], BF16) for _ in range(NB)]
        r_g = [Res("agate%d" % i) for i in range(NB)]
        den = [A.alloc([128, 4], F32) for _ in range(NB)]
        tmpo = [A.alloc([128, 4, 64], F32) for _ in range(NB)]
        mixo = [A.alloc([128, 256], BF16) for _ in range(NB)]
        r_fin = [Res("afin%d" % i) for i in range(NB)]
        r_mx = [Res("amix%d" % i) for i in range(NB)]
        units = []
        for (nm, t0, ntl) in self.seqs():
            for qi in range(ntl):
                T = t0 + qi
                if nm == "s" and self.half(l) and qi >= ntl // 2:
                    continue
                if nm == "s":
                    chunks = []
                    if qi > 0:
                        chunks.append((T - 1, "prev"))
                    chunks.append((T, None))
                    if qi < ntl - 1:
                        chunks.append((T + 1, "next"))
                    chunks += [(NT + a, "ctx") for a in range(4)]
                else:
                    chunks = [(t0 + a, None) for a in range(ntl)]
                for h in range(4):
                    units.append((T, h, chunks))
        NPT = 3
        PT = [A.alloc([128, 7, 128], BF16) for _ in range(NPT)]
        r_pt = [Res("aPT%d" % i) for i in range(NPT)]

        def qk(u):
            T, h, chunks = units[u]
            qc = 0 if h in (0, 2) else 1
            pb0 = 0 if h < 2 else 64
            sbk = (0, 2, 6)[u % 3]
            sps = self.psum[:, sbk * 512:(sbk + 2) * 512]
            for ci, (kt, kind) in enumerate(chunks):
                if kind == "ctx":
                    lhsT = KTc[pb0:pb0 + 64, (kt - NT) * 128:(kt - NT + 1) * 128]
                    rk = r_kc
                else:
                    lhsT = KT[pb0:pb0 + 64, kt * 128:(kt + 1) * 128]
                    rk = r_k
                bk = sbk + (ci * 128) // 512
                self.mm(sps[:, ci * 128:(ci + 1) * 128], lhsT, QT[pb0:pb0 + 64, qc, T * 128:(T + 1) * 128], True, True,
                        [rk, r_q], [pres[bk]], tile_position=(pb0, 0))

        qk(0)
        qk(1)
        for u, (T, h, chunks) in enumerate(units):
            b = T % NB
            if h == 0:
                self.dma("sp", gate[b], S["SG"][T * 128:(T + 1) * 128, 0:256], writes=[r_g[b]])
            g = h // 2
            sbk = (0, 2, 6)[u % 3]
            pbuf = u % NPT
            sps = self.psum[:, sbk * 512:(sbk + 2) * 512]
            ob = 4 + (T % 2)
            O = self.bank(ob)[:, 0:260].rearrange("p (h d) -> p h d", d=65)
            nch = len(chunks)
            ptv = PT[pbuf][:, 0:nch, :]
            rb = [pres[sbk]] + ([pres[sbk + 1]] if nch > 4 else [])
            self.act(ptv, sps[:, 0:nch * 128].rearrange("p (c q) -> p c q", q=128), AF.Exp, rb, [r_pt[pbuf]], scale=0.125)
            for ci, (kt, kind) in enumerate(chunks):
                if kind == "prev":
                    self.tt("dve", PT[pbuf][:, ci, :], PT[pbuf][:, ci, :], mprev, ALU.mult, [r_pt[pbuf], r_m], [r_pt[pbuf]])
                elif kind == "next":
                    self.tt("dve", PT[pbuf][:, ci, :], PT[pbuf][:, ci, :], mnext, ALU.mult, [r_pt[pbuf], r_m], [r_pt[pbuf]])
            if u + 2 < len(units):
                qk(u + 2)
            for ci, (kt, kind) in enumerate(chunks):
                self.mm(O[:, h, :], PT[pbuf][:, ci, :], Vaug[:, kt, g, :], ci == 0, ci == nch - 1,
                        [r_pt[pbuf], r_v], [pres[ob]])
            if h == 3:
                self.tt("dve", den[b], O[:, :, 64], esink, ALU.add, [pres[ob], r_es], [r_fin[b]])
                self.recip(den[b], den[b], [r_fin[b]], [r_fin[b]])
                self.tt("dve", tmpo[b], O[:, :, 0:64], den[b].unsqueeze(2).broadcast_to([128, 4, 64]), ALU.mult, [pres[ob], r_fin[b]], [r_fin[b]])
                self.tt("dve", mixo[b], tmpo[b].rearrange("p h d -> p (h d)"), gate[b], ALU.mult, [r_fin[b], r_g[b]], [r_mx[b]])
                self.dma(STQ, S["MIX"][T * 128:(T + 1) * 128, 0:256], mixo[b], reads=[r_mx[b]])
        A.release()

    def attnB(self, l):
        A, R, I, S = self.A, self.R, self.inp, self.S
        pres = self.pres
        lam_init = 0.8 - 0.6 * math.exp(-0.3 * l)
        A.mark()
        QTm = [A.alloc([128, 2, TOK], BF16) for _ in range(4)]
        KT = A.alloc([128, 2, TOK], BF16)
        KTc = A.alloc([128, 2, 512], BF16)
        Vaug = A.alloc([128, NT + 4, 4, 128], BF16)
        r_q = Res("bQT"); r_k = Res("bKT"); r_kc = Res("bKTc"); r_v = Res("bV")
        QKTv = S["QKT"].rearrange("(c p) t -> p c t", p=128)
        for h in range(4):
            self.memset("dve" if h % 2 == 0 else "pool", QTm[h], 0.0, [r_q] if h == 0 else [])
        R.barrier()
        for h in range(4):
            self.dma("sp", QTm[h][32 * h:32 * h + 32, :, :], QKTv[32 * h:32 * h + 32, 3:5, :], writes=[r_q], accum=True)
        self.dma("sp", KT, QKTv[:, 5:7, :], writes=[r_k])
        lam = A.alloc([128, 4, 32], F32)
        lp = A.alloc([128, 2, 32], F32)
        ls = A.alloc([128, 2], F32)
        nlam = A.alloc([128, 1], F32)
        r_lam = Res("lam")
        self.dma("sp", lam, I["diff_lambda"][l:l + 1, :].broadcast_to([128, 128]).rearrange("p (a b) -> p a b", b=32), writes=[r_lam])
        lam4 = lam.rearrange("p (a two) d -> p a two d", two=2)
        self.tt("dve", lp, lam4[:, :, 0, :], lam4[:, :, 1, :], ALU.mult, [r_lam], [r_lam])
        R.op("dve", lambda e: e.tensor_reduce(out=ls, in_=lp, axis=AX.X, op=ALU.add), reads=[r_lam], writes=[r_lam])
        self.act(ls, ls, AF.Exp, [r_lam], [r_lam])
        self.tt("dve", nlam, ls[:, 1:2], ls[:, 0:1], ALU.subtract, [r_lam], [r_lam])
        self.ts("dve", nlam, nlam, -lam_init, None, ALU.add, ALU.bypass, [r_lam], [r_lam])
        sl = A.alloc([128, 64], F32)
        r_sl = Res("subln")
        self.dma("sp", sl, I["diff_subln"][l:l + 1, :].broadcast_to([128, 64]), writes=[r_sl])
        self.ts("dve", sl, sl, 1.0 - lam_init, None, ALU.mult, ALU.bypass, [r_sl], [r_sl])
        A.mark()
        vraw = A.alloc([128, NT, 256], BF16)
        vcraw = A.alloc([128, 4, 256], F32)
        r_vr = Res("bvraw")
        self.dma("sp", vraw, S["VS"][:, 128:384].rearrange("(a p) c -> p a c", p=128), writes=[r_vr])
        R.op("sp", lambda e: e.dma_start(out=vcraw, in_=I["cdv"][l].rearrange("(a p) c -> p a c", p=128)), writes=[r_vr], dma=True, accum=True)
        self.memset("pool", Vaug, 0.0, [r_v])
        self.memset("pool", Vaug[:, :, :, 64:65], 1.0, [r_v])
        self.cp("dve", Vaug[:, 0:NT, :, 0:64], vraw.rearrange("p a (g d) -> p a g d", d=64), [r_vr], [r_v])
        self.cp("dve", Vaug[:, NT:NT + 4, :, 0:64], vcraw.rearrange("p a (g d) -> p a g d", d=64), [r_vr], [r_v])
        self.load_ctx_T(I["cdk"][l], 256, KTc, r_kc, 7)
        R.barrier()
        A.release()
        PT = [A.alloc([128, 2, 512], BF16) for _ in range(3)]
        r_pt = [Res("bPT%d" % i) for i in range(3)]
        identf = A.alloc([128, 128], F32)
        r_idf = Res("identf")
        self.dma("sp", identf, I["identf"], writes=[r_idf])
        OT = [A.alloc([65, 4, 512], F32) for _ in range(2)]
        r_ot = [Res("bOT%d" % i) for i in range(2)]
        NB = 2
        gate = [A.alloc([128, 256], BF16) for _ in range(NB)]
        r_g = [Res("bgate%d" % i) for i in range(NB)]
        rd = [A.alloc([128, 2, 4], F32) for _ in range(NB)]
        ta = [A.alloc([128, 4, 64], F32) for _ in range(NB)]
        tb = [A.alloc([128, 4, 64], F32) for _ in range(NB)]
        ssq = [A.alloc([128, 4], F32) for _ in range(NB)]
        mixo = [A.alloc([128, 256], BF16) for _ in range(NB)]
        r_f = [Res("bfin%d" % i) for i in range(NB)]
        r_mx = [Res("bmix%d" % i) for i in range(NB)]
        rounds = []
        for (nm, t0, ntl) in self.seqs():
            W = 512 if nm == "s" else 256
            ktiles = list(range(t0, t0 + ntl)) + ([NT + a for a in range(4)] if nm == "s" else [])
            nqc = ntl * 128 // W
            if nm == "s" and self.half(l):
                nqc //= 2
            for qc in range(nqc):
                q0 = t0 * 128 + qc * W
                for m in range(2):
                    for hp in range(2):
                        rounds.append((q0, W, m, hp, ktiles))
        stages = []
        for ri, (q0, W, m, hp, ktiles) in enumerate(rounds):
            for ki, kt in enumerate(ktiles):
                stages.append((ri, ki, kt))

        def qk(si):
            ri, ki, kt = stages[si]
            q0, W, m, hp, ktiles = rounds[ri]
            sb = (si % 2) * 2
            for hh in range(2):
                h = hp * 2 + hh
                if kt >= NT:
                    lhsT = KTc[:, m, (kt - NT) * 128:(kt - NT + 1) * 128]
                    rk = r_kc
                else:
                    lhsT = KT[:, m, kt * 128:(kt + 1) * 128]
                    rk = r_k
                self.mm(self.bank(sb + hh)[:, 0:W], lhsT, QTm[h][:, m, q0:q0 + W], True, True,
                        [rk, r_q], [pres[sb + hh]])

        def fin(qs, q0, ob):
            T = q0 // 128 + qs
            b = T % NB

            self.dma("sp", gate[b], S["SG"][T * 128:(T + 1) * 128, 256:512], writes=[r_g[b]])
            ov = []
            for mm_ in range(2):
                tbk = ob + mm_
                for h in range(4):
                    self.tr(self.bank(tbk)[:, h * 65:(h + 1) * 65], OT[mm_][:, h, qs * 128:(qs + 1) * 128], [r_ot[mm_], r_idf], [pres[tbk]], ident=identf)
                ov.append(self.bank(tbk)[:, 0:260].rearrange("p (h d) -> p h d", d=65))
            o0, o1 = ov
            self.recip(rd[b][:, 0, :], o0[:, :, 64], [pres[ob]], [r_f[b]])
            self.recip(rd[b][:, 1, :], o1[:, :, 64], [pres[ob + 1]], [r_f[b]])
            self.ts("dve", rd[b][:, 1, :], rd[b][:, 1, :], nlam, None, ALU.mult, ALU.bypass, [r_f[b], r_lam], [r_f[b]])
            self.tt("dve", ta[b], o0[:, :, 0:64], rd[b][:, 0, :].unsqueeze(2).broadcast_to([128, 4, 64]), ALU.mult, [pres[ob], r_f[b]], [r_f[b]])
            self.tt("dve", tb[b], o1[:, :, 0:64], rd[b][:, 1, :].unsqueeze(2).broadcast_to([128, 4, 64]), ALU.mult, [pres[ob + 1], r_f[b]], [r_f[b]])
            self.tt("dve", ta[b], ta[b], tb[b], ALU.add, [r_f[b]], [r_f[b]])
            self.tt("dve", tb[b], ta[b], ta[b], ALU.mult, [r_f[b]], [r_f[b]])
            R.op("dve", lambda e, b=b: e.tensor_reduce(out=ssq[b], in_=tb[b], axis=AX.X, op=ALU.add), reads=[r_f[b]], writes=[r_f[b]])
            self.ts("dve", ssq[b], ssq[b], 1.0 / 64, EPS, ALU.mult, ALU.add, [r_f[b]], [r_f[b]])
            self.tt("pool", ssq[b], ssq[b], self.neghalf.broadcast_to([128, 4]), ALU.pow, [r_f[b], self.r_nh], [r_f[b]])
            self.tt("dve", ta[b], ta[b], ssq[b].unsqueeze(2).broadcast_to([128, 4, 64]), ALU.mult, [r_f[b]], [r_f[b]])
            self.tt("dve", ta[b], ta[b], sl.unsqueeze(1).broadcast_to([128, 4, 64]), ALU.mult, [r_f[b], r_sl], [r_f[b]])
            self.tt("dve", mixo[b], ta[b].rearrange("p h d -> p (h d)"), gate[b], ALU.mult, [r_f[b], r_g[b]], [r_mx[b]])
            self.dma(STQ, S["MIX"][T * 128:(T + 1) * 128, 256:512], mixo[b], reads=[r_mx[b]])


        pending = []

        qk(0)
        qk(1)
        for si, (ri, ki, kt) in enumerate(stages):
            q0, W, m, hp, ktiles = rounds[ri]
            nq = W // 128
            if len(ktiles) < 16:
                while pending:
                    pending.pop(0)[1]()
            elif pending and si >= pending[0][0]:
                pending.pop(0)[1]()
            sb = (si % 2) * 2
            pbuf = si % 3
            ob = 4 + (ri % 2) * 2
            src = self.psum[:, sb * 512:(sb + 2) * 512].rearrange("p (h q) -> p h q", q=512)[:, :, 0:W]
            self.act(PT[pbuf][:, :, 0:W], src, AF.Exp, [pres[sb], pres[sb + 1]], [r_pt[pbuf]], scale=32.0 ** -0.5)
            if si + 2 < len(stages):
                qk(si + 2)
            for hh in range(2):
                h = hp * 2 + hh
                self.mm(self.bank(ob + hh)[:, 0:W], Vaug[:, kt, h, :], PT[pbuf][:, hh, 0:W], ki == 0, ki == len(ktiles) - 1,
                        [r_pt[pbuf], r_v], [pres[ob + hh]])
            if ki != len(ktiles) - 1:
                continue
            srcO = self.psum[0:65, ob * 512:(ob + 2) * 512].rearrange("p (h q) -> p h q", q=512)[:, :, 0:W]
            self.cp("dve", OT[m][:, 2 * hp:2 * hp + 2, 0:W], srcO, [pres[ob], pres[ob + 1]], [r_ot[m]])
            if not (m == 1 and hp == 1):
                continue
            defer = (len(ktiles) >= 16) and os.environ.get("BDEFER", "1") == "1"
            for qs in range(nq):
                if defer:
                    pending.append((si + 6 + 5 * qs, (lambda qs=qs, q0=q0, ob=ob: fin(qs, q0, ob))))
                else:
                    fin(qs, q0, ob)
        while pending:
            pending.pop(0)[1]()
        A.release()

    def fnet(self, l):
        A, R, I, S = self.A, self.R, self.inp, self.S
        pres = self.pres
        A.mark()
        fw = A.alloc([128, 2, 256], F32)
        bdc = A.alloc([128, 128], F32); bds = A.alloc([128, 128], F32)
        Wt = [A.alloc([128, 2, 256], BF16) for _ in range(2)]
        fnb = A.alloc([128, 256], F32)
        r_c = Res("fconst"); r_W = Res("fW")
        self.dma("sp", fw, I["fn_w"][l].rearrange("(a p) c -> p a c", p=128), writes=[r_c])
        R.op("sp", lambda e: e.dma_start(out=bdc, in_=I["bdc"]), writes=[r_c], dma=True, accum=True)
        R.op("sp", lambda e: e.dma_start(out=bds, in_=I["bds"]), writes=[r_c], dma=True, accum=True)
        R.op("sp", lambda e: e.dma_start(out=fnb, in_=I["fn_b"][l:l + 1, :].broadcast_to([128, 256])), writes=[r_c], dma=True, accum=True)
        for ti, bd in enumerate((bdc, bds)):
            for ch in range(2):
                bk = ti * 2 + ch
                self.mm(self.bank(bk)[:, 0:256], bd, fw[:, ch, :], True, True, [r_c], [pres[bk]])
                self.cp("dve", Wt[ti][:, ch, :], self.bank(bk)[:, 0:256], [pres[bk]], [r_W])
        KW = 256
        NB = 3
        tab = [[A.alloc([128, 32, KW], BF16) for _ in range(2)] for _ in range(NB)]
        r_tab = [Res("ftab%d" % i) for i in range(NB)]
        AT = [A.alloc([128, 2, 2, KW], BF16) for _ in range(NB)]
        r_at = [Res("fAT%d" % i) for i in range(NB)]
        gate = [A.alloc([128, 256], BF16) for _ in range(NB)]
        r_g = [Res("fgate%d" % i) for i in range(NB)]
        tmp = [A.alloc([128, 256], F32) for _ in range(NB)]
        mixo = [A.alloc([128, 256], BF16) for _ in range(NB)]
        r_t = [Res("ftmp%d" % i) for i in range(NB)]
        r_mx = [Res("fmix%d" % i) for i in range(NB)]
        u = A.alloc([128, 32, 256], BF16)
        r_u = Res("fu")
        kcc = 0
        for (nm, t0, ntl) in self.seqs():
            n = ntl * 128
            tag = "s" if nm == "s" else "p"
            scale = 1.0 / math.sqrt(n * 64.0)
            self.dma("sp", u[:, 0:ntl, :], S["FU"][t0 * 128:(t0 + ntl) * 128, :].rearrange("(a p) c -> p a c", p=128), writes=[r_u])
            nkc = n // KW
            if nm == "s" and self.half(l):
                nkc //= 2
            for kc in range(nkc):
                tb_ = kcc % NB
                kcc += 1
                self.dma("sp", tab[tb_][0][:, 0:ntl, :], I["fc" + tag][kc], writes=[r_tab[tb_]])
                R.op("sp", lambda e, tb_=tb_, kc=kc, tag=tag, ntl=ntl: e.dma_start(out=tab[tb_][1][:, 0:ntl, :], in_=I["fs" + tag][kc]),
                     writes=[r_tab[tb_]], dma=True, accum=True)
                for ti in range(2):
                    for ch in range(2):
                        bk = ti
                        for tc in range(ntl):
                            self.mm(self.bank(bk)[:, ch * 256:(ch + 1) * 256], u[:, tc, ch * 128:(ch + 1) * 128], tab[tb_][ti][:, tc, :],
                                    tc == 0, tc == ntl - 1, [r_u, r_tab[tb_]], [pres[bk]])
                    self.cp("act", AT[tb_][:, ti, :, :], self.bank(ti).rearrange("p (c k) -> p c k", k=KW), [pres[ti]], [r_at[tb_]])
                for ks in range(KW // 128):
                    T = t0 + (kc * KW) // 128 + ks
                    b = T % NB
                    ob = 2 + (T % 2)
                    self.dma("sp", gate[b], S["SG"][T * 128:(T + 1) * 128, 512:768], writes=[r_g[b]])
                    i = 0
                    for ti in range(2):
                        for ch in range(2):
                            self.mm(self.bank(ob)[:, 0:256], AT[tb_][:, ti, ch, ks * 128:(ks + 1) * 128], Wt[ti][:, ch, :], i == 0, i == 3,
                                    [r_at[tb_], r_W], [pres[ob]])
                            i += 1
                    self.stt(tmp[b], self.bank(ob)[:, 0:256], scale, fnb, ALU.mult, ALU.add, [pres[ob], r_c], [r_t[b]])
                    self.tt("dve", mixo[b], tmp[b], gate[b], ALU.mult, [r_t[b], r_g[b]], [r_mx[b]])
                    self.dma(STQ, S["MIX"][T * 128:(T + 1) * 128, 768:1024], mixo[b], reads=[r_mx[b]])
        A.release()

    def sin3(self, out, arg, tmp, reads, r_tmp, writes):
        s_, q_ = tmp
        self.act(s_, arg, AF.Sin, reads, [r_tmp], scale=1.0 / 3.0)
        self.tt("dve", q_, s_, s_, ALU.mult, [r_tmp], [r_tmp])
        self.ts("dve", q_, q_, -4.0, 3.0, ALU.mult, ALU.add, [r_tmp], [r_tmp])
        self.tt("dve", out, q_, s_, ALU.mult, [r_tmp], writes)

    def hyena(self, l):
        A, R, I, S = self.A, self.R, self.inp, self.S
        pres = self.pres
        A.mark()
        cw = A.alloc([128, 6, 4], F32)
        skip = A.alloc([128, 2], F32)
        w1 = A.alloc([33, 64], F32); w2 = A.alloc([64, 64], F32); w3 = A.alloc([64, 512], F32)
        b1 = A.alloc([64, 1], F32); b2 = A.alloc([64, 1], F32); fr = A.alloc([64, 1], F32)
        ones = A.alloc([128, 128], F32)
        r_p = Res("hparams")
        first = True
        for dst, src in ((cw, I["hcw"][l]), (skip, I["hskip"][l]), (w1, I["hw1"][l]), (w2, I["hw2"][l]), (w3, I["hw3"][l]),
                         (b1, I["hb1"][l]), (b2, I["hb2"][l]), (fr, I["hfreq"][l]), (ones, I["ones"])):
            self.dma("sp", dst, src, writes=[r_p], accum=not first)
            first = False
        R.barrier()
        groups = [("s", 0, 32, 1), ("p", 32, 2, NPR)]
        for (tag, t0, ntl, nseq) in groups:
            n = ntl * 128
            N = 2 * n
            ntt = ntl * nseq
            A.mark()
            Ztok = A.alloc([128, ntt, 256], BF16)
            M1 = A.alloc([128, ntt, 256], BF16)
            M2 = A.alloc([128, ntt, 256], BF16)
            r_z = Res("hZ"); r_m1 = Res("hM1"); r_m2 = Res("hM2")
            A.mark()
            ntok = ntt * 128
            c0 = t0 * 128
            HUTv = S["HUT"].rearrange("(c p) t -> p c t", p=128)
            SGHTv = S["SGHT"].rearrange("(c p) t -> p c t", p=128)
            hu = A.alloc([128, 3, ntok], BF16)
            sgh = A.alloc([128, ntok], BF16)
            uc = [A.alloc([128, ntok], F32) for _ in range(3)]
            zT = A.alloc([128, ntok], BF16); m1T = A.alloc([128, ntok], BF16); m2T = A.alloc([128, ntok], BF16)
            r_hu = Res("hhu"); r_uc = [Res("huc%d" % i) for i in range(3)]; r_fm = Res("hfm")
            for ch in range(2):
                for j in range(3):
                    self.dma("sp", hu[:, j, :], HUTv[:, 2 * j + ch, c0:c0 + ntok], writes=[r_hu], accum=(j > 0))
                self.dma("sp", sgh, SGHTv[:, ch, c0:c0 + ntok], writes=[r_hu], accum=True)
                for j in range(3):
                    cj = 2 * j + ch
                    self.ts("dve", uc[j], hu[:, j, :], cw[:, cj, 1:2], cw[:, cj, 3:4], ALU.mult, ALU.add, [r_hu, r_p], [r_uc[j]])
                    for sq in range(nseq):
                        a = sq * n
                        self.stt(uc[j][:, a + 1:a + n], hu[:, j, a:a + n - 1], cw[:, cj, 0:1], uc[j][:, a + 1:a + n], ALU.mult, ALU.add,
                                 [r_hu, r_p, r_uc[j]], [r_uc[j]])
                        self.stt(uc[j][:, a:a + n - 1], hu[:, j, a + 1:a + n], cw[:, cj, 2:3], uc[j][:, a:a + n - 1], ALU.mult, ALU.add,
                                 [r_hu, r_p, r_uc[j]], [r_uc[j]])
                self.tt("dve", zT, uc[1], uc[2], ALU.mult, [r_uc[1], r_uc[2]], [r_fm])
                self.tt("pool", m1T, uc[0], sgh, ALU.mult, [r_uc[0], r_hu], [r_fm])
                self.stt(m2T, zT, skip[:, ch:ch + 1], m1T, ALU.mult, ALU.mult, [r_fm, r_p], [r_fm])
                for ti in range(ntt):
                    bk = ti % 2
                    pb = self.bank(bk, BF16)
                    for k, srcT in enumerate((zT, m1T, m2T)):
                        self.tr(pb[:, k * 128:(k + 1) * 128], srcT[:, ti * 128:(ti + 1) * 128], [r_fm], [pres[bk]])
                    self.cp("act", Ztok[:, ti, ch * 128:(ch + 1) * 128], pb[:, 0:128], [pres[bk]], [r_z])
                    self.cp("act", M1[:, ti, ch * 128:(ch + 1) * 128], pb[:, 128:256], [pres[bk]], [r_m1])
                    self.cp("dve", M2[:, ti, ch * 128:(ch + 1) * 128], pb[:, 256:384], [pres[bk]], [r_m2])
            R.barrier()
            A.release()
            ka = A.alloc([128, ntl, 256], BF16)
            kb = A.alloc([128, ntl, 256], BF16)
            r_ka = Res("hka")
            h0bc = A.alloc([128, 256], F32)
            r_h0 = Res("hh0")
            rot = A.alloc([128, ntl, 3], F32)
            r_rot = Res("hrot")
            self.dma("sp", rot, I["rot" + tag], writes=[r_rot])
            A.mark()
            hraw = A.alloc([128, ntl, 512], BF16)
            r_hr = Res("hraw")
            acc = A.alloc([128, 512], F32)
            r_acc = Res("hacc")
            self.memset("pool", acc, 0.0, [r_acc])
            CW = min(512, n)
            NBF = 2
            ft = [A.alloc([33, CW], F32) for _ in range(NBF)]
            r_ft = [Res("hft%d" % i) for i in range(NBF)]
            a1 = A.alloc([64, CW], F32); h1 = A.alloc([64, CW], F32); a2 = A.alloc([64, CW], F32); h2 = A.alloc([64, CW], F32)
            sc = [A.alloc([64, CW], F32) for _ in range(2)]
            r_a = Res("ha"); r_h1 = Res("hh1"); r_h2 = Res("hh2"); r_sc = Res("hsc")
            dec = [A.alloc([128, 256], F32) for _ in range(NBF)]
            r_dec = [Res("hdec%d" % i) for i in range(NBF)]
            ab = A.alloc([128, 512], F32)
            r_ab = Res("hab")
            fb1 = A.alloc([64, 1], F32); fb2 = A.alloc([64, 1], F32)
            r_fb = Res("hfb")
            self.tt("dve", fb1, b1, fr, ALU.mult, [r_p], [r_fb])
            self.tt("dve", fb2, b2, fr, ALU.mult, [r_p], [r_fb])
            for cc in range(n // CW):
                fb = cc % NBF
                self.dma("sp", ft[fb], I["feat" + tag][:, cc * CW:(cc + 1) * CW], writes=[r_ft[fb]])
                self.mm(self.bank(2)[0:64, 0:CW], w1, ft[fb], True, True, [r_p, r_ft[fb]], [pres[2]])
                self.ts("dve", a1, self.bank(2)[0:64, 0:CW], fr, fb1, ALU.mult, ALU.add, [pres[2], r_p, r_fb], [r_a])
                self.sin3(h1, a1, sc, [r_a], r_sc, [r_h1])
                self.mm(self.bank(3)[0:64, 0:CW], w2, h1, True, True, [r_p, r_h1], [pres[3]])
                self.ts("dve", a2, self.bank(3)[0:64, 0:CW], fr, fb2, ALU.mult, ALU.add, [pres[3], r_p, r_fb], [r_a])
                self.sin3(h2, a2, sc, [r_a], r_sc, [r_h2])
                for tt_ in range(CW // 128):
                    ti = cc * (CW // 128) + tt_
                    bk = 4 + ti % 2
                    db = ti % NBF
                    self.dma("sp", dec[db], I["dec" + tag][ti * 128:(ti + 1) * 128, :], writes=[r_dec[db]])
                    self.mm(self.bank(bk), h2[:, tt_ * 128:(tt_ + 1) * 128], w3, True, True, [r_h2, r_p], [pres[bk]])
                    self.tt("dve", hraw[:, ti, :].rearrange("p (d c) -> p d c", c=256), self.bank(bk).rearrange("p (d c) -> p d c", c=256),
                            dec[db].unsqueeze(1).broadcast_to([128, 2, 256]), ALU.mult, [pres[bk], r_dec[db]], [r_hr])
                    self.act(ab, hraw[:, ti, :], AF.Abs, [r_hr], [r_ab])
                    self.tt("pool", acc, acc, ab, ALU.add, [r_ab, r_acc], [r_acc])
            rn = A.alloc([128, 256], F32)
            r_rn = Res("hrn")
            self.mm(self.bank(6), ones, acc, True, True, [r_p, r_acc], [pres[6]])
            self.cp("dve", ab, self.bank(6), [pres[6]], [r_ab])
            self.tt("dve", rn, ab[:, 0:256], ab[:, 256:512], ALU.add, [r_ab], [r_rn])
            self.ts("dve", rn, rn, EPS, None, ALU.add, ALU.bypass, [r_rn], [r_rn])
            self.recip(rn, rn, [r_rn], [r_rn])
            hb_st = [A.alloc([128, 256], F32) for _ in range(2)]
            r_hbst = [Res("hbst%d" % i) for i in range(2)]
            hbs = [A.alloc([128, 256], F32) for _ in range(2)]
            r_hbs = [Res("hbs%d" % i) for i in range(2)]
            hfn = [A.alloc([128, 256], F32) for _ in range(2)]
            r_hfn = [Res("hfn%d" % i) for i in range(2)]
            for ti in range(ntl):
                b = ti % 2
                self.tt("dve", hb_st[b], hraw[:, ti, 256:512], rn, ALU.mult, [r_hr, r_rn], [r_hbst[b]])
                self.dma(STQ, S["HB"][ti * 128:(ti + 1) * 128, :], hb_st[b], reads=[r_hbst[b]])
            R.barrier()
            for ti in range(ntl):
                b = ti % 2
                if ti == ntl - 1:
                    self.memset("pool", hbs[b], 0.0, [r_hbs[b]])
                    self.dma("sp", hbs[b][0:127, :], S["HB"][ti * 128 + 1:(ti + 1) * 128, :], writes=[r_hbs[b]])
                else:
                    self.dma("sp", hbs[b], S["HB"][ti * 128 + 1:(ti + 1) * 128 + 1, :], writes=[r_hbs[b]])
                self.tt("dve", hfn[b], hraw[:, ti, 0:256], rn, ALU.mult, [r_hr, r_rn], [r_hfn[b]])
                if ti == 0:
                    hb0 = A.alloc([1, 256], F32)
                    h0row = A.alloc([1, 256], F32)
                    hfl = A.alloc([1, 2], F32)
                    r_h0r = Res("hh0row")
                    self.dma("sp", hfl, I["hflag"], writes=[r_h0r])
                    self.dma("sp", hb0, S["HB"][0:1, :], writes=[r_h0r], accum=True)
                    self.ts("dve", h0row, hfn[b][0:1, :], hfl[:, 0:1], None, ALU.mult, ALU.bypass, [r_hfn[b], r_h0r], [r_h0r])
                    self.stt(h0row, hb0, hfl[:, 1:2], h0row, ALU.mult, ALU.add, [r_h0r], [r_h0r])
                    self.mm(self.bank(7)[:, 0:256], ones[0:1, :], h0row, True, True, [r_p, r_h0r], [pres[7]])
                    self.cp("dve", h0bc, self.bank(7)[:, 0:256], [pres[7]], [r_h0])
                    self.memset("dve", hfn[b][0:1, :], 0.0, [r_hfn[b]])
                self.tt("dve", ka[:, ti, :], hfn[b], hbs[b], ALU.add, [r_hfn[b], r_hbs[b]], [r_ka])
                self.tt("pool", kb[:, ti, :], hfn[b], hbs[b], ALU.subtract, [r_hfn[b], r_hbs[b]], [r_ka])
            R.barrier()
            A.release()
            A.mark()
            Pr = A.alloc([128, ntl, nseq * 256], BF16)
            nPi = A.alloc([128, ntl, nseq * 256], BF16)
            r_P = Res("hP")
            NBP = 3
            pan = [[A.alloc([128, ntl, 128], BF16) for _ in range(2)] for _ in range(NBP)]
            r_pan = [Res("hpan%d" % i) for i in range(NBP)]
            Kr = A.alloc([128, 256], F32); Ki = A.alloc([128, 256], F32); t1 = A.alloc([128, 256], F32); t2 = A.alloc([128, 256], F32)
            r_K = Res("hK"); r_t = Res("ht")
            u1 = A.alloc([128, 256], F32); u2 = A.alloc([128, 256], F32)
            for kt in range(ntl):
                pb_ = kt % NBP
                self.dma("sp", pan[pb_][0], I["hc" + tag][kt], writes=[r_pan[pb_]])
                self.dma("sp", pan[pb_][1], I["hs" + tag][kt], writes=[r_pan[pb_]], accum=True)
                for ti_, rhs in enumerate((ka, kb)):
                    for tc in range(ntl):
                        self.mm(self.bank(0)[:, ti_ * 256:(ti_ + 1) * 256], pan[pb_][ti_][:, tc, :], rhs[:, tc, :], tc == 0, tc == ntl - 1,
                                [r_pan[pb_], r_ka], [pres[0]])
                Kc = self.bank(0)[:, 0:256]
                Ks = self.bank(0)[:, 256:512]
                rc = rot[:, kt, 0:1]; rs = rot[:, kt, 1:2]; nrc = rot[:, kt, 2:3]
                self.ts("dve", t1, Kc, rc, None, ALU.mult, ALU.bypass, [pres[0], r_rot], [r_t])
                self.stt(Kr, Ks, rs, t1, ALU.mult, ALU.add, [pres[0], r_rot, r_t], [r_K])
                self.ts("dve", t2, Kc, rs, None, ALU.mult, ALU.bypass, [pres[0], r_rot], [r_t])
                self.stt(Ki, Ks, nrc, t2, ALU.mult, ALU.add, [pres[0], r_rot, r_t], [r_K])
                for sq in range(nseq):
                    bk = 1 + (kt * nseq + sq) % 2
                    for ti_ in range(2):
                        for tc in range(ntl):
                            self.mm(self.bank(bk)[:, ti_ * 256:(ti_ + 1) * 256], pan[pb_][ti_][:, tc, :], Ztok[:, sq * ntl + tc, :], tc == 0, tc == ntl - 1,
                                    [r_pan[pb_], r_z], [pres[bk]])
                    Zc = self.bank(bk)[:, 0:256]
                    Zs = self.bank(bk)[:, 256:512]
                    osl = slice(sq * 256, (sq + 1) * 256)
                    self.tt("dve", u1, Zc, Kr, ALU.mult, [pres[bk], r_K], [r_t])
                    self.tt("dve", u2, Zs, Ki, ALU.mult, [pres[bk], r_K], [r_t])
                    self.tt("pool", Pr[:, kt, osl], u1, u2, ALU.add, [r_t], [r_P])
                    self.tt("dve", t1, Zs, Kr, ALU.mult, [pres[bk], r_K], [r_t])
                    self.tt("dve", t2, Zc, Ki, ALU.mult, [pres[bk], r_K], [r_t])
                    self.tt("pool", nPi[:, kt, osl], t1, t2, ALU.subtract, [r_t], [r_P])
            R.barrier()
            NBO = 2
            tmpy = [A.alloc([128, 256], F32) for _ in range(NBO)]
            mixo = [A.alloc([128, 256], BF16) for _ in range(NBO)]
            r_ty = [Res("hty%d" % i) for i in range(NBO)]
            r_mx = [Res("hmx%d" % i) for i in range(NBO)]
            oc = 0
            ntl_out = ntl // 2 if (tag == "s" and self.half(l)) else ntl
            hz = [A.alloc([128, 256], F32) for _ in range(NBO)]
            r_hz = [Res("hhz%d" % i) for i in range(NBO)]
            for tt_ in range(ntl_out):
                pb_ = tt_ % NBP
                self.dma("sp", pan[pb_][0], I["hc" + tag][tt_], writes=[r_pan[pb_]])
                self.dma("sp", pan[pb_][1], I["hs" + tag][tt_], writes=[r_pan[pb_]], accum=True)
                for sq in range(nseq):
                    bk = 4 + oc % 2
                    b = oc % NBO
                    oc += 1
                    osl = slice(sq * 256, (sq + 1) * 256)
                    i = 0
                    for ti_, src in enumerate((Pr, nPi)):
                        for kt in range(ntl):
                            self.mm(self.bank(bk)[:, 0:256], pan[pb_][ti_][:, kt, :], src[:, kt, osl], i == 0, i == 2 * ntl - 1,
                                    [r_pan[pb_], r_P], [pres[bk]])
                            i += 1
                    gt = sq * ntl + tt_
                    T = t0 + gt
                    self.tt("pool", hz[b], Ztok[:, gt, :], h0bc, ALU.mult, [r_z, r_h0], [r_hz[b]])
                    self.stt(tmpy[b], self.bank(bk)[:, 0:256], 2.0 / N, hz[b], ALU.mult, ALU.add, [pres[bk], r_hz[b]], [r_ty[b]])
                    self.tt("dve", tmpy[b], tmpy[b], M1[:, gt, :], ALU.mult, [r_ty[b], r_m1], [r_ty[b]])
                    self.tt("dve", mixo[b], tmpy[b], M2[:, gt, :], ALU.add, [r_ty[b], r_m2], [r_mx[b]])
                    self.dma(STQ, S["MIX"][T * 128:(T + 1) * 128, 512:768], mixo[b], reads=[r_mx[b]])
            R.barrier()
            A.release()
            A.release()
        A.release()

    def exit(self, l):
        A, R, I, S = self.A, self.R, self.inp, self.S
        A.mark()
        wout = A.alloc([128, 8, D], BF16)
        r_w = Res("wout")
        src = I["w_out"][l].rearrange("(k p) n -> p k n", p=128)
        for k in range(8):
            R.op("pool", lambda e, k=k: e.dma_start(out=wout[:, k, :], in_=src[:, k, :]), writes=[r_w], dma=True, accum=True)
        NB = 2
        mt = [A.alloc([128, D], BF16) for _ in range(NB)]
        r_mt = [Res("mt%d" % i) for i in range(NB)]
        mT = [A.alloc([128, 8, 128], BF16) for _ in range(NB)]
        r_mT = [Res("mT%d" % i) for i in range(NB)]
        xt = [A.alloc([128, D], F32) for _ in range(NB)]
        r_xt = [Res("ext%d" % i) for i in range(NB)]
        junk = A.alloc([128, D], BF16)
        r_junk = Res("ejunk")
        ss = [A.alloc([128, 1], F32) for _ in range(NB)]
        rstd = [A.alloc([128, 1], F32) for _ in range(NB)]
        r_ss = [Res("ess%d" % i) for i in range(NB)]
        tmp = [A.alloc([128, D], F32) for _ in range(NB)]
        r_tmp = [Res("etmp%d" % i) for i in range(NB)]
        xn = [A.alloc([128, D], F32) for _ in range(NB)]
        r_xn = [Res("xn%d" % i) for i in range(NB)]
        pres = self.pres
        tiles = list(range(NT))
        if self.half(l):
            tiles = list(range(16)) + list(range(32, NT))
        for T in tiles:
            cond = 0 if T < 32 else 1
            b = T % NB
            self.dma("sp", mt[b], S["MIX"][T * 128:(T + 1) * 128, :], writes=[r_mt[b]])
            self.dma("sp", xt[b], self.xsrc(l, T), writes=[r_xt[b]])
            pbk = 4 + (T % 2)
            pb = self.bank(pbk, BF16)
            for k in range(8):
                self.tr(pb[:, k * 128:(k + 1) * 128], mt[b][:, k * 128:(k + 1) * 128], [r_mt[b]], [pres[pbk]])
            self.cp("act", mT[b], pb.rearrange("p (k t) -> p k t", t=128), [pres[pbk]], [r_mT[b]])
            ob = (T % 2) * 2
            for c in range(2):
                for k in range(8):
                    self.mm(self.bank(ob + c), mT[b][:, k, :], wout[:, k, c * 512:(c + 1) * 512], k == 0, k == 7,
                            [r_mT[b], r_w], [pres[ob + c]])
            o = self.psum[:, ob * 512:(ob + 2) * 512]
            R.op("act", lambda e, b=b, o=o: e.activation(out=junk, in_=o, func=AF.Square, accum_out=ss[b]),
                 reads=[pres[ob], pres[ob + 1]], writes=[r_junk, r_ss[b]])
            self.ts("dve", ss[b], ss[b], 1.0 / D, EPS, ALU.mult, ALU.add, [r_ss[b]], [r_ss[b]])
            self.tt("pool", rstd[b], ss[b], self.neghalf, ALU.pow, [r_ss[b], self.r_nh], [r_ss[b]])
            self.stt(tmp[b], o, rstd[b], self.GG[cond], ALU.mult, ALU.mult, [pres[ob], pres[ob + 1], r_ss[b], self.r_mod], [r_tmp[b]])
            self.tt("dve", xn[b], tmp[b], xt[b], ALU.add, [r_tmp[b], r_xt[b]], [r_xn[b]])
            if l == DEPTH - 1:
                if T < 32:
                    dst = self.out["ys"][T * 128:(T + 1) * 128, :]
                else:
                    dst = self.out["yp"][(T - 32) * 128:(T - 31) * 128, :]
            else:
                dst = S["xcur"][T * 128:(T + 1) * 128, :]
            self.dma(STQ, dst, xn[b], reads=[r_xn[b]])
        A.release()


_CACHE = {}


def _get_program(debug=(), nlayers=DEPTH, phases=None):
    key = (tuple(debug), nlayers, phases)
    if key not in _CACHE:
        _CACHE[key] = Builder(debug, nlayers, phases).build()
    return _CACHE[key]


def _host_inputs(inputs):
    f = lambda a: np.ascontiguousarray(np.asarray(a, dtype=np.float32))
    perm = _win_perm()
    w_in = f(inputs["w_in"])[:, :, perm]
    consts = _consts()
    shared = {
        "w_ada": f(inputs["w_ada"]), "b_ada": f(inputs["b_ada"]),
        "norm_pre": f(inputs["norm_pre"]), "norm_post": f(inputs["norm_post"]),
        "w_in": np.ascontiguousarray(w_in), "w_out": f(inputs["w_out"]),
    }
    shared.update(consts)
    shared["attn_sink"] = f(inputs["attn_sink"])
    shared["diff_lambda"] = f(inputs["diff_lambda"]).reshape(DEPTH, 128)
    shared["diff_subln"] = f(inputs["diff_subln"])
    cw = f(inputs["hy_conv_w"])
    cb = f(inputs["hy_conv_b"])
    hcw = np.concatenate([cw, cb[:, None, :]], axis=1)
    shared["hcw"] = np.ascontiguousarray(hcw.reshape(DEPTH, 4, 6, 128).transpose(0, 3, 2, 1))
    shared["hskip"] = np.ascontiguousarray(f(inputs["hy_skip"]).reshape(DEPTH, 2, 128).transpose(0, 2, 1))
    shared["hw1"] = f(inputs["hy_filt_w1"])
    shared["hb1"] = f(inputs["hy_filt_b1"]).reshape(DEPTH, 64, 1)
    shared["hw2"] = f(inputs["hy_filt_w2"])
    shared["hb2"] = f(inputs["hy_filt_b2"]).reshape(DEPTH, 64, 1)
    shared["hw3"] = f(inputs["hy_filt_w3"])
    shared["hfreq"] = f(inputs["hy_filt_freq"]).reshape(DEPTH, 64, 1)
    shared["fn_w"] = f(inputs["fn_w"])
    shared["fn_b"] = f(inputs["fn_b"])
    xs = f(inputs["x_sample"])
    xp = f(inputs["x_prompt"])
    c = f(inputs["c"])
    cctx = f(inputs["c_ctx"])
    cak = f(inputs["cache_attn_k"]); cav = f(inputs["cache_attn_v"])
    cdk = f(inputs["cache_diff_k"]); cdv = f(inputs["cache_diff_v"])
    maps = []
    rev_over = {}
    for k_ in ("fcs", "fss", "fcp", "fsp"):
        rev_over[k_] = shared.pop(k_ + "r")
    rev_over["rope"] = np.ascontiguousarray(shared["rope"][::-1])
    hw3 = shared["hw3"]
    rev_over["hw3"] = np.ascontiguousarray(np.concatenate([hw3[:, :, 256:], hw3[:, :, :256]], axis=2))
    rev_over["hcw"] = np.ascontiguousarray(shared["hcw"][:, :, :, [2, 1, 0, 3]])
    rev_over["hflag"] = np.array([[0.0, 1.0]], np.float32)
    shared["hflag"] = np.array([[1.0, 0.0]], np.float32)
    for core in range(NCORES):
        b = core // 2
        m = dict(shared)
        if core % 2 == 1:
            m.update(rev_over)
            m["xs"] = np.ascontiguousarray(xs[b][::-1])
            m["xp"] = np.ascontiguousarray(xp[4 * core:4 * core + 4][:, ::-1].reshape(NPR * LP, D))
        else:
            m["xs"] = xs[b]
            m["xp"] = np.ascontiguousarray(xp[4 * core:4 * core + 4].reshape(NPR * LP, D))
        cond = np.stack([c[b], cctx], axis=-1)
        m["condT"] = np.ascontiguousarray(cond.reshape(8, 128, 2).transpose(1, 0, 2))
        m["cak"] = np.ascontiguousarray(cak[b].reshape(DEPTH, PAST, 128))
        m["cav"] = np.ascontiguousarray(cav[b].reshape(DEPTH, PAST, 128))
        m["cdk"] = np.ascontiguousarray(cdk[b].reshape(DEPTH, PAST, 256))
        m["cdv"] = np.ascontiguousarray(cdv[b].reshape(DEPTH, PAST, 256))
        maps.append(m)
    return maps


def _run(inputs, debug=(), nlayers=DEPTH, phases=None, cores=None, trace=False):
    nc = _get_program(debug, nlayers, phases)
    maps = _host_inputs(inputs)
    if cores is not None:
        maps = [maps[c] for c in cores]
    if trace:
        res = run_bass_kernel_spmd(nc, maps, core_ids=list(range(len(maps))), trace=True)
        return res
    res = run_bass_kernel_spmd(nc, maps, core_ids=list(range(len(maps))))
    return res.results


def kernel(**inputs):
    r = _run(inputs)

    def fix(i, a, axis):
        return np.flip(a, axis=axis) if i % 2 == 1 else a

    yp = np.concatenate([fix(i, r[i]["yp"].reshape(NPR, LP, D), 1) for i in range(NCORES)], axis=0)
    ys = np.stack([np.concatenate([r[2 * b]["ys"], r[2 * b + 1]["ys"][::-1]], axis=0) for b in range(4)], axis=0)
    nak = np.concatenate([fix(i, r[i]["nak"].reshape(NPR, DEPTH, LP, 2, 64), 2) for i in range(NCORES)], axis=0)
    nav = np.concatenate([fix(i, r[i]["nav"].reshape(NPR, DEPTH, LP, 2, 64), 2) for i in range(NCORES)], axis=0)
    ndk = np.concatenate([fix(i, r[i]["ndk"].reshape(NPR, DEPTH, LP, 2, 4, 32), 2) for i in range(NCORES)], axis=0)
    ndv = np.concatenate([fix(i, r[i]["ndv"].reshape(NPR, DEPTH, LP, 4, 64), 2) for i in range(NCORES)], axis=0)
    return (yp.astype(np.float32), ys.astype(np.float32), nak.astype(np.float32), nav.astype(np.float32),
            ndk.astype(np.float32), ndv.astype(np.float32))
```

```python
import math
import os
LVL = int(os.environ.get('ENTRY_LVL', '99'))
STQ = os.environ.get('STQ', 'pool')
SUB = os.environ.get('ENTRY_SUB', 'cde')
STS = os.environ.get('ENTRY_STS', '')
import numpy as np
import ml_dtypes
import concourse.bass as bass
import concourse.mybir as mybir
from concourse.bass_utils import run_bass_kernel_spmd

F32 = mybir.dt.float32
BF16 = mybir.dt.bfloat16
U8 = mybir.dt.uint8
AF = mybir.ActivationFunctionType
ALU = mybir.AluOpType
AX = mybir.AxisListType

NCORES = 8
D = 1024
DEPTH = 2
LS = 4096
LP = 256
NPR = 4
TOK = LS + NPR * LP
NT = TOK // 128
NST = TOK // 512
PAST = 512
EPS = 1e-6
DIN = 3328
NTM = 2304
NFM = 1024

C_AQ, C_AK, C_DQ, C_DK, C_AV, C_DV, C_AG, C_DG, C_FU, C_FG = 0, 256, 384, 640, 896, 1024, 1280, 1536, 1792, 2048


class Res:
    __slots__ = ("name", "w", "r", "excl")

    def __init__(self, name, excl=False):
        self.name = name
        self.excl = excl
        self.w = []
        self.r = []


class Op:
    __slots__ = ("eng", "fn", "deps", "dma", "dsem", "dval", "signal", "count", "pre")

    def __init__(self, eng, fn, dma):
        self.eng = eng
        self.fn = fn
        self.dma = dma
        self.deps = []
        self.dsem = None
        self.dval = 0
        self.signal = False
        self.count = 0
        self.pre = None


ENGS = ("pe", "act", "dve", "pool", "sp")


class Rec:
    def __init__(self, nc):
        self.nc = nc
        self.ops = {e: [] for e in ENGS}
        self.csem = {e: nc.alloc_semaphore("c_" + e) for e in ENGS}
        self.npool = {"sp": 40, "pool": 40, "act": 16}
        self.dpool = {q: [nc.alloc_semaphore("d_%s_%d" % (q, i)) for i in range(n)] for q, n in self.npool.items()}
        self.ndma = {q: 0 for q in self.npool}
        self.alldma = []

    def op(self, eng, fn, reads=(), writes=(), dma=False, accum=False):
        o = Op(eng, fn, dma)
        deps = []
        for r in reads:
            for d in r.w:
                deps.append(d)
            if r.excl:
                for d in r.r:
                    if d.dma or d.eng != eng:
                        deps.append(d)
        for w in writes:
            for d in w.w:
                if accum and (not d.dma) and (not dma) and d.eng == eng:
                    continue
                if accum and d.dma and dma:
                    continue
                deps.append(d)
            for d in w.r:
                if (not d.dma) and (not dma) and d.eng == eng:
                    continue
                deps.append(d)
        seen = set()
        for d in deps:
            if d is o or id(d) in seen:
                continue
            seen.add(id(d))
            if (not d.dma) and (not dma) and d.eng == eng and eng == "pe":
                continue
            o.deps.append(d)
        if dma:
            i = self.ndma[eng]
            self.ndma[eng] = i + 1
            P = self.npool[eng]
            o.dsem = self.dpool[eng][i % P]
            o.dval = 16 * (i // P + 1)
            if i >= P:
                o.pre = (o.dsem, 16 * (i // P))
            self.alldma.append(o)
        for r in reads:
            if not dma:
                r.r = [x for x in r.r if x.dma or x.eng != eng]
            r.r.append(o)
        for w in writes:
            if dma:
                if accum:
                    w.w = w.w + [o]
                else:
                    w.w = [o]
            else:
                if accum:
                    w.w = [x for x in w.w if x.dma or x.eng != eng] + [o]
                else:
                    w.w = [x for x in w.w if (not x.dma) and x.eng != eng] + [o]
            w.r = []
        self.ops[eng].append(o)
        return o

    def barrier(self):
        lasts = []
        for e in ENGS:
            for o in reversed(self.ops[e]):
                if not o.dma and o.fn is not None:
                    lasts.append(o)
                    break
        pend = list(self.alldma)
        self.alldma = []
        for e in ENGS:
            o = Op(e, None, False)
            o.deps = [d for d in lasts if d.eng != e] + pend
            self.ops[e].append(o)

    def emit(self):
        nc = self.nc
        for e in ENGS:
            for o in self.ops[e]:
                for d in o.deps:
                    if not d.dma:
                        d.signal = True
        for e in ENGS:
            c = 0
            for o in self.ops[e]:
                if o.signal and not o.dma:
                    c += 1
                    o.count = c
        final_waits = []
        for q in self.npool:
            n = self.ndma[q]
            P = self.npool[q]
            for j in range(min(n, P)):
                cnt = (n - 1 - j) // P + 1
                final_waits.append((self.dpool[q][j], 16 * cnt))

        def run(ename, eng):
            seen = {}
            for o in self.ops[ename]:
                waits = []
                if o.pre is not None:
                    waits.append(o.pre)
                for d in o.deps:
                    if d.dma:
                        waits.append((d.dsem, d.dval))
                    else:
                        waits.append((self.csem[d.eng], d.count))
                for (s, v) in waits:
                    k = id(s)
                    if seen.get(k, 0) >= v:
                        continue
                    seen[k] = v
                    eng.wait_ge(s, v)
                if o.fn is None:
                    continue
                ins = o.fn(eng)
                if o.dma:
                    ins.then_inc(o.dsem, 16)
                elif o.signal:
                    ins.then_inc(self.csem[ename], 1)
            if ename == "sp":
                for (s, v) in final_waits:
                    if seen.get(id(s), 0) < v:
                        eng.wait_ge(s, v)

        with nc.Block() as blk:
            @blk.tensor
            def _(e):
                run("pe", e)

            @blk.scalar
            def _(e):
                run("act", e)

            @blk.vector
            def _(e):
                run("dve", e)

            @blk.gpsimd
            def _(e):
                run("pool", e)

            @blk.sync
            def _(e):
                run("sp", e)


class Arena:
    def __init__(self, nc, nbytes):
        self.t = nc.alloc_sbuf_tensor("arena", [128, nbytes], U8)
        self.nbytes = nbytes
        self.top = 0
        self.marks = []

    def alloc(self, shape, dtype):
        esz = 2 if dtype == BF16 else 4
        n = 1
        for s in shape[1:]:
            n *= s
        nb = (n * esz + 63) // 64 * 64
        assert self.top + nb <= self.nbytes, ("arena overflow", self.top, nb)
        ap = self.t[:, self.top:self.top + n * esz].bitcast(dtype)
        self.top += nb
        if len(shape) == 3:
            ap = ap.rearrange("p (a b) -> p a b", b=shape[2])
        elif len(shape) == 4:
            ap = ap.rearrange("p (a b c) -> p a b c", b=shape[2], c=shape[3])
        if shape[0] < 128:
            ap = ap[0:shape[0]]
        return ap

    def mark(self):
        self.marks.append(self.top)

    def release(self):
        self.top = self.marks.pop()


def _rope_tables(n, head_dim):
    pos = np.arange(n)
    row = (pos // 64).astype(np.float32)
    col = (pos % 64).astype(np.float32)
    nf = head_dim // 4
    inv = (10000.0 ** (-np.arange(nf, dtype=np.float32) / nf)).astype(np.float32)
    ang = np.concatenate([row[:, None] * inv, col[:, None] * inv], axis=-1).astype(np.float32)
    return np.cos(ang).astype(np.float32), np.sin(ang).astype(np.float32)


def _consts():
    c = {}
    ca, sa = _rope_tables(LS, 64)
    cd, sd = _rope_tables(LS, 32)
    c["rope"] = np.concatenate([ca, sa, cd, sd], axis=1).astype(np.float32)
    sel = np.zeros((2, 256), np.float32)
    sel[0, :128] = 1.0
    sel[1, 128:] = 1.0
    c["sel"] = sel
    c["ident"] = np.eye(128, dtype=np.float32).astype(ml_dtypes.bfloat16)
    c["identf"] = np.eye(128, dtype=np.float32)
    j = np.arange(128)
    c["mprev"] = (j[:, None] >= j[None, :]).astype(np.float32).astype(ml_dtypes.bfloat16)
    c["mnext"] = (j[:, None] <= j[None, :]).astype(np.float32).astype(ml_dtypes.bfloat16)
    jj = np.arange(64)
    blk = 2.0 * np.pi * jj[:, None] * jj[None, :] / 64.0
    bc = np.zeros((128, 128), np.float32)
    bs = np.zeros((128, 128), np.float32)
    for g in range(2):
        bc[g * 64:(g + 1) * 64, g * 64:(g + 1) * 64] = np.cos(blk)
        bs[g * 64:(g + 1) * 64, g * 64:(g + 1) * 64] = -np.sin(blk)
    c["bdc"] = bc
    c["bds"] = bs
    c["ones"] = np.ones((128, 128), np.float32)
    for n, tag in ((LS, "s"), (LP, "p")):
        nt = n // 128
        N = 2 * n
        t = np.arange(n, dtype=np.float64)
        KW = 256
        th = 2.0 * np.pi * np.outer(t, t) / n
        for nm, fn in (("fc", np.cos), ("fs", np.sin)):
            m = fn(th).astype(np.float32)
            for rv in (0, 1):
                mm_ = m[::-1, ::-1] if rv else m
                mm_ = mm_.reshape(nt, 128, n // KW, KW).transpose(2, 1, 0, 3)
                c[nm + tag + ("r" if rv else "")] = np.ascontiguousarray(mm_).astype(ml_dtypes.bfloat16)
        th = 2.0 * np.pi * np.outer(t + 0.5, t + 0.5) / N
        for nm, fn in (("hc", np.cos), ("hs", np.sin)):
            m = fn(th).astype(np.float32)
            m = m.reshape(nt, 128, nt, 128).transpose(2, 1, 0, 3)
            c[nm + tag] = np.ascontiguousarray(m).astype(ml_dtypes.bfloat16)
        ang = 2.0 * np.pi * (t + 0.5) / N / 2.0
        rot = np.stack([np.cos(ang), np.sin(ang), -np.cos(ang)], axis=-1).astype(np.float32)
        c["rot" + tag] = np.ascontiguousarray(rot.reshape(nt, 128, 3).transpose(1, 0, 2))
        tl = np.linspace(0.0, 1.0, n, dtype=np.float32)[:, None]
        w = (2.0 * math.pi / n) * np.arange(n, dtype=np.float32)[:, None]
        f = np.linspace(1e-4, 15, 16, dtype=np.float32)[None, :]
        feats = np.concatenate([tl, np.cos(f * w), -np.sin(f * w)], axis=-1).astype(np.float32)
        c["feat" + tag] = np.ascontiguousarray(feats.T)
        hmin = math.log(1e-2) / 1.5
        hmax = math.log(1e-2) / 0.3
        deltas = np.abs(np.linspace(hmin, hmax, 256, dtype=np.float32))
        c["dec" + tag] = np.exp(-tl * deltas).astype(np.float32)
    return c


WIN_PERM = None


def _win_perm():
    aq = np.arange(0, 256).reshape(4, 64)[[0, 2, 1, 3]].reshape(-1)
    ak = np.arange(256, 384)
    av = np.arange(384, 512)
    ag = np.arange(512, 768)
    dq = np.arange(768, 1024)
    dk = np.arange(1024, 1280)
    dv = np.arange(1280, 1536)
    dg = np.arange(1536, 1792)
    hu = np.arange(1792, 2560)
    hg = np.arange(2560, 2816)
    fu = np.arange(2816, 3072)
    fg = np.arange(3072, 3328)
    return np.concatenate([aq, ak, dq, dk, av, dv, ag, dg, fu, fg, hu, hg])


class Builder:
    def __init__(self, debug=None, nlayers=DEPTH, phases=None):
        self.debug = debug or ()
        self.nlayers = nlayers
        self.phases = phases or ("setup", "entry", "mix", "exit")
        nc = bass.Bass("TRN2", target_bir_lowering=False)
        self.nc = nc
        self.R = Rec(nc)
        self.A = Arena(nc, 200 * 1024)
        self.psum = nc.alloc_psum_tensor("ps", [128, 8 * 512], F32)
        self.pres = [Res("ps%d" % i, excl=True) for i in range(8)]
        self.inp = {}
        self.out = {}

        def din(name, shape, dt=F32):
            self.inp[name] = nc.dram_tensor(name, list(shape), dt, kind="ExternalInput").ap()

        def dout(name, shape, dt=F32):
            self.out[name] = nc.dram_tensor(name, list(shape), dt, kind="ExternalOutput").ap()

        def dscr(name, shape, dt):
            return nc.dram_tensor(name, list(shape), dt, kind="Internal").ap()

        din("xs", [LS, D])
        din("xp", [NPR * LP, D])
        din("condT", [128, 8, 2])
        din("cak", [DEPTH, PAST, 128])
        din("cav", [DEPTH, PAST, 128])
        din("cdk", [DEPTH, PAST, 256])
        din("cdv", [DEPTH, PAST, 256])
        din("w_ada", [DEPTH, D, 3 * D])
        din("b_ada", [DEPTH, 3 * D])
        din("norm_pre", [DEPTH, D])
        din("norm_post", [DEPTH, D])
        din("w_in", [DEPTH, D, DIN])
        din("w_out", [DEPTH, D, D])
        din("rope", [LS, 96])
        din("sel", [2, 256])
        din("ident", [128, 128], BF16)
        din("identf", [128, 128])
        din("mprev", [128, 128], BF16)
        din("mnext", [128, 128], BF16)
        din("bdc", [128, 128])
        din("bds", [128, 128])
        din("ones", [128, 128])
        for n, tag in ((LS, "s"), (LP, "p")):
            nt = n // 128
            din("fc" + tag, [n // 256, 128, nt, 256], BF16)
            din("fs" + tag, [n // 256, 128, nt, 256], BF16)
            din("hc" + tag, [nt, 128, nt, 128], BF16)
            din("hs" + tag, [nt, 128, nt, 128], BF16)
            din("rot" + tag, [128, nt, 3])
            din("feat" + tag, [33, n])
            din("dec" + tag, [n, 256])
        din("attn_sink", [DEPTH, 4])
        din("diff_lambda", [DEPTH, 128])
        din("diff_subln", [DEPTH, 64])
        din("hcw", [DEPTH, 128, 6, 4])
        din("hskip", [DEPTH, 128, 2])
        din("hw1", [DEPTH, 33, 64])
        din("hb1", [DEPTH, 64, 1])
        din("hw2", [DEPTH, 64, 64])
        din("hb2", [DEPTH, 64, 1])
        din("hw3", [DEPTH, 64, 512])
        din("hfreq", [DEPTH, 64, 1])
        din("fn_w", [DEPTH, 256, 256])
        din("fn_b", [DEPTH, 256])
        din("hflag", [1, 2])
        dout("ys", [LS // 2, D])
        dout("yp", [NPR * LP, D])
        dout("nak", [NPR, DEPTH, LP, 128])
        dout("nav", [NPR, DEPTH, LP, 128])
        dout("ndk", [NPR, DEPTH, LP, 256])
        dout("ndv", [NPR, DEPTH, LP, 256])
        S = {}
        S["xcur"] = dscr("xcur", [TOK, D], F32)
        S["QKT"] = dscr("QKT", [896, TOK], BF16)
        S["VS"] = dscr("VS", [TOK, 384], BF16)
        S["SG"] = dscr("SG", [TOK, 768], BF16)
        S["FU"] = dscr("FU", [TOK, 256], BF16)
        S["HUT"] = dscr("HUT", [768, TOK], BF16)
        S["SGHT"] = dscr("SGHT", [256, TOK], BF16)
        S["MIX"] = dscr("MIX", [TOK, D], BF16)
        S["HB"] = dscr("HB", [LS, 256], F32)
        self.S = S
        for nm in self.debug:
            src = S[nm]
            dout("dbg_" + nm, list(src.shape), BF16 if src.dtype == BF16 else F32)

    def bank(self, i, dt=F32):
        ap = self.psum[:, i * 512:(i + 1) * 512]
        if dt == BF16:
            ap = ap.bitcast(BF16)
        return ap

    def dma(self, q, out, in_, reads=(), writes=(), accum=False):
        return self.R.op(q, lambda e, out=out, in_=in_: e.dma_start(out=out, in_=in_), reads=reads, writes=writes, dma=True, accum=accum)

    def memset(self, eng, ap, val, writes):
        return self.R.op(eng, lambda e, ap=ap, val=val: e.memset(ap, val), writes=writes)

    def recip(self, out, in_, reads, writes):
        return self.R.op("dve", lambda e, out=out, in_=in_: e.reciprocal(out=out, in_=in_), reads=reads, writes=writes)

    def mm(self, out, lhsT, rhs, start, stop, reads, writes, tile_position=None):
        kw = {}
        if tile_position is not None:
            kw["tile_position"] = tile_position
        return self.R.op("pe", lambda e: e.matmul(out, lhsT, rhs, start=start, stop=stop, **kw),
                         reads=reads, writes=writes, accum=not start)

    def tr(self, out, in_, reads, writes, ident=None):
        ident = self.ident if ident is None else ident
        p = in_.shape[0]
        return self.R.op("pe", lambda e: e.transpose(out, in_, ident[0:p, 0:p]), reads=list(reads) + [self.r_ident], writes=writes, accum=True)

    def act(self, out, in_, func, reads, writes, **kw):
        return self.R.op("act", lambda e: e.activation(out=out, in_=in_, func=func, **kw), reads=reads, writes=writes)

    def tt(self, eng, out, in0, in1, op, reads, writes):
        return self.R.op(eng, lambda e: e.tensor_tensor(out=out, in0=in0, in1=in1, op=op), reads=reads, writes=writes)

    def ts(self, eng, out, in0, s1, s2, op0, op1, reads, writes):
        return self.R.op(eng, lambda e: e.tensor_scalar(out=out, in0=in0, scalar1=s1, scalar2=s2, op0=op0, op1=op1), reads=reads, writes=writes)

    def stt(self, out, in0, scalar, in1, op0, op1, reads, writes):
        return self.R.op("dve", lambda e: e.scalar_tensor_tensor(out=out, in0=in0, scalar=scalar, in1=in1, op0=op0, op1=op1), reads=reads, writes=writes)

    def cp(self, eng, out, in_, reads, writes):
        if eng == "act":
            return self.R.op("act", lambda e: e.activation(out=out, in_=in_, func=AF.Copy), reads=reads, writes=writes)
        return self.R.op(eng, lambda e: e.tensor_copy(out=out, in_=in_), reads=reads, writes=writes)

    def build(self):
        A, R = self.A, self.R
        I, S = self.inp, self.S
        self.ident = A.alloc([128, 128], BF16)
        self.r_ident = Res("ident")
        self.dma("sp", self.ident, I["ident"], writes=[self.r_ident])
        self.sel = A.alloc([2, 256], F32)
        self.r_sel = Res("sel")
        self.dma("sp", self.sel, I["sel"], writes=[self.r_sel])
        self.neghalf = A.alloc([128, 1], F32)
        self.r_nh = Res("nh")
        R.op("dve", lambda e: e.memset(self.neghalf, -0.5), writes=[self.r_nh])
        condT = A.alloc([128, 8, 2], F32)
        self.scond = A.alloc([128, 8, 2], BF16)
        r_c = Res("condT")
        self.r_scond = Res("scond")
        self.dma("sp", condT, I["condT"], writes=[r_c])
        self.act(self.scond, condT, AF.Silu, [r_c], [self.r_scond])
        self.GG = [A.alloc([128, D], F32) for _ in range(2)]
        self.r_mod = Res("mod")
        for l in range(self.nlayers):
            A.mark()
            self.SH = [A.alloc([128, D], F32) for _ in range(2)]
            self.GS = [A.alloc([128, D], F32) for _ in range(2)]
            if "setup" in self.phases:
                self.layer_setup(l)
                R.barrier()
            if "entry" in self.phases:
                self.entry(l)
                R.barrier()
            A.release()
            if "A" in self.phases or "mix" in self.phases:
                self.attnA(l)
                R.barrier()
            if "B" in self.phases or "mix" in self.phases:
                self.attnB(l)
                R.barrier()
            if "F" in self.phases or "mix" in self.phases:
                self.fnet(l)
                R.barrier()
            if "H" in self.phases or "mix" in self.phases:
                self.hyena(l)
                R.barrier()
            if "exit" in self.phases:
                self.exit(l)
                R.barrier()
        for nm in self.debug:
            self.dma("sp", self.out["dbg_" + nm], S[nm])
        R.emit()
        return self.nc

    def layer_setup(self, l):
        A, R, I = self.A, self.R, self.inp
        A.mark()
        wada = A.alloc([128, 8, 3 * D], BF16)
        r_wada = Res("wada")
        src = I["w_ada"][l].rearrange("(k p) n -> p k n", p=128)
        for k in range(8):
            for h in range(2):
                self.R.op("pool", lambda e, k=k, h=h: e.dma_start(out=wada[:, k, h * 1536:(h + 1) * 1536], in_=src[:, k, h * 1536:(h + 1) * 1536]),
                          writes=[r_wada], dma=True, accum=True)
        bada = A.alloc([2, 3 * D], F32)
        r_b = Res("bada")
        self.dma("sp", bada, I["b_ada"][l:l + 1, :].broadcast_to([2, 3 * D]), writes=[r_b])
        gpre = A.alloc([128, D], F32)
        gpost = A.alloc([128, D], F32)
        r_g = Res("gprepost")
        self.dma("sp", gpre, I["norm_pre"][l:l + 1, :].broadcast_to([128, D]), writes=[r_g])
        R.op("sp", lambda e: e.dma_start(out=gpost, in_=I["norm_post"][l:l + 1, :].broadcast_to([128, D])), writes=[r_g], dma=True, accum=True)
        modrow = A.alloc([2, 3 * D], F32)
        r_mr = Res("modrow")
        for c in range(6):
            pb = self.bank(c)[0:2, :]
            for k in range(8):
                self.mm(pb, self.scond[:, k, :], wada[:, k, c * 512:(c + 1) * 512], k == 0, k == 7,
                        [self.r_scond, r_wada], [self.pres[c]])
            self.tt("dve", modrow[:, c * 512:(c + 1) * 512], pb, bada[:, c * 512:(c + 1) * 512], ALU.add,
                    [self.pres[c], r_b], [r_mr])
        R.barrier()
        i = 0
        for cond in range(2):
            lhsT = self.sel[:, cond * 128:(cond + 1) * 128]
            for part in range(3):
                for h in range(2):
                    b = i % 8
                    i += 1
                    pb = self.bank(b)
                    self.mm(pb, lhsT, modrow[:, part * D + h * 512: part * D + (h + 1) * 512], True, True,
                            [self.r_sel, r_mr], [self.pres[b]])
                    sl = slice(h * 512, (h + 1) * 512)
                    if part == 0:
                        self.cp("dve", self.SH[cond][:, sl], pb, [self.pres[b]], [self.r_mod])
                    elif part == 1:
                        self.stt(self.GS[cond][:, sl], pb, 1.0, gpre[:, sl], ALU.add, ALU.mult, [self.pres[b], r_g], [self.r_mod])
                    else:
                        self.tt("dve", self.GG[cond][:, sl], pb, gpost[:, sl], ALU.mult, [self.pres[b], r_g], [self.r_mod])
        A.release()

    def xsrc(self, l, T):
        if l == 0:
            if T < 32:
                return self.inp["xs"][T * 128:(T + 1) * 128, :]
            return self.inp["xp"][(T - 32) * 128:(T - 31) * 128, :]
        return self.S["xcur"][T * 128:(T + 1) * 128, :]

    def entry(self, l):
        A, R, I, S = self.A, self.R, self.inp, self.S
        A.mark()
        win = A.alloc([128, 8, DIN], BF16)
        r_win = Res("win")
        src = I["w_in"][l].rearrange("(k p) n -> p k n", p=128)
        for k in range(8):
            for h in range(2):
                R.op("pool", lambda e, k=k, h=h: e.dma_start(out=win[:, k, h * 1664:(h + 1) * 1664], in_=src[:, k, h * 1664:(h + 1) * 1664]),
                     writes=[r_win], dma=True, accum=True)
        NB = 2
        xt = [A.alloc([128, D], F32) for _ in range(NB)]
        r_xt = [Res("xt%d" % i) for i in range(NB)]
        junk = A.alloc([128, D], BF16)
        r_junk = Res("junk")
        ss = [A.alloc([128, 1], F32) for _ in range(NB)]
        rstd = [A.alloc([128, 1], F32) for _ in range(NB)]
        r_ss = [Res("ss%d" % i) for i in range(NB)]
        tmp = A.alloc([128, D], F32)
        r_tmp = Res("tmp")
        hb = [A.alloc([128, D], BF16) for _ in range(NB)]
        r_hb = [Res("hb%d" % i) for i in range(NB)]
        hT = [A.alloc([128, 8, 512], BF16) for _ in range(2)]
        r_hT = [Res("hT%d" % i) for i in range(2)]
        rope = [A.alloc([128, 96], F32) for _ in range(NB)]
        r_rope = [Res("rope%d" % i) for i in range(NB)]
        rq = [A.alloc([128, 896], BF16) for _ in range(NB)]
        r_rq = [Res("rq%d" % i) for i in range(NB)]
        rt = [A.alloc([128, 256], F32) for _ in range(4)]
        r_rt = [Res("rt%d" % i) for i in range(4)]
        qkt_st = [A.alloc([128, 7, 512], BF16) for _ in range(2)]
        vs_st = [A.alloc([128, 4, 384], BF16) for _ in range(2)]
        sg_st = [A.alloc([128, 4, 768], BF16) for _ in range(2)]
        fu_st = [A.alloc([128, 4, 256], BF16) for _ in range(2)]
        hut_st = [A.alloc([128, 6, 512], BF16) for _ in range(2)]
        sght_st = [A.alloc([128, 2, 512], BF16) for _ in range(2)]
        r_qkt = [Res("qkt%d" % i) for i in range(2)]
        r_vs = [Res("vs%d" % i) for i in range(2)]
        r_sg = [Res("sg%d" % i) for i in range(2)]
        r_fu = [Res("fu%d" % i) for i in range(2)]
        r_hut = [Res("hut%d" % i) for i in range(2)]
        r_sght = [Res("sght%d" % i) for i in range(2)]
        cst = [A.alloc([128, 768], F32) for _ in range(2)]
        r_cst = [Res("cst%d" % i) for i in range(2)]
        pres = self.pres
        QKTv = S["QKT"].rearrange("(c p) t -> p c t", p=128)
        HUTv = S["HUT"].rearrange("(c p) t -> p c t", p=128)
        SGHTv = S["SGHT"].rearrange("(c p) t -> p c t", p=128)
        ps = self.psum

        def stageA(T):
            st, tt_ = T // 4, T % 4
            cond = 0 if st < 8 else 1
            sb = st % 2
            b = T % NB
            self.dma("sp", xt[b], self.xsrc(l, T), writes=[r_xt[b]])
            if cond == 0:
                self.dma("sp", rope[b], I["rope"][T * 128:(T + 1) * 128, :], writes=[r_rope[b]])
            R.op("act", lambda e, b=b: e.activation(out=junk, in_=xt[b], func=AF.Square, accum_out=ss[b]),
                 reads=[r_xt[b]], writes=[r_junk, r_ss[b]])
            self.ts("dve", ss[b], ss[b], 1.0 / D, EPS, ALU.mult, ALU.add, [r_ss[b]], [r_ss[b]])
            self.tt("pool", rstd[b], ss[b], self.neghalf, ALU.pow, [r_ss[b], self.r_nh], [r_ss[b]])
            self.stt(tmp, xt[b], rstd[b], self.GS[cond], ALU.mult, ALU.mult, [r_xt[b], r_ss[b], self.r_mod], [r_tmp])
            self.tt("dve", hb[b], tmp, self.SH[cond], ALU.add, [r_tmp, self.r_mod], [r_hb[b]])
            pb = self.bank(5, BF16)
            for k in range(8):
                self.tr(pb[:, k * 128:(k + 1) * 128], hb[b][:, k * 128:(k + 1) * 128], [r_hb[b]], [pres[5]])
            self.cp("act", hT[sb][:, :, tt_ * 128:(tt_ + 1) * 128], pb.rearrange("p (k t) -> p k t", t=128),
                    [pres[5]], [r_hT[sb]])

        def stageB(T):
            st, tt_ = T // 4, T % 4
            sb = st % 2
            for c in range(5):
                w = 512 if c < 4 else 256
                for k in range(8):
                    self.mm(self.bank(c)[:, 0:w], hT[sb][:, k, tt_ * 128:(tt_ + 1) * 128], win[:, k, c * 512:c * 512 + w],
                            k == 0, k == 7, [r_hT[sb], r_win], [pres[c]])

        def stageC(T):
            st, tt_ = T // 4, T % 4
            cond = 0 if st < 8 else 1
            sb = st % 2
            b = T % NB
            if cond == 0:
                cosA = rope[b][:, 0:32].unsqueeze(1).broadcast_to([128, 6, 32])
                sinA = rope[b][:, 32:64].unsqueeze(1).broadcast_to([128, 6, 32])
                cosD = rope[b][:, 64:80].unsqueeze(1).broadcast_to([128, 16, 16])
                sinD = rope[b][:, 80:96].unsqueeze(1).broadcast_to([128, 16, 16])
                for (c0, nh, hd, cs, sn, banks) in ((0, 6, 64, cosA, sinA, [0]), (384, 16, 32, cosD, sinD, [0, 1])):
                    half = hd // 2
                    src3 = ps[:, c0:c0 + nh * hd].rearrange("p (h d) -> p h d", d=hd)
                    dst3 = rq[b][:, c0:c0 + nh * hd].rearrange("p (h d) -> p h d", d=hd)
                    x1 = src3[:, :, 0:half]
                    x2 = src3[:, :, half:hd]
                    n = nh * half
                    t = [rt[i][:, 0:n].rearrange("p (h d) -> p h d", d=half) for i in range(4)]
                    rb = [pres[i] for i in banks]
                    self.tt("dve", t[0], x1, cs, ALU.mult, rb + [r_rope[b]], [r_rt[0]])
                    self.tt("dve", t[1], x2, sn, ALU.mult, rb + [r_rope[b]], [r_rt[1]])
                    self.tt("dve", t[2], x1, sn, ALU.mult, rb + [r_rope[b]], [r_rt[2]])
                    self.tt("dve", t[3], x2, cs, ALU.mult, rb + [r_rope[b]], [r_rt[3]])
                    self.tt("pool", dst3[:, :, 0:half], t[0], t[1], ALU.subtract, [r_rt[0], r_rt[1]], [r_rq[b]])
                    self.tt("pool", dst3[:, :, half:hd], t[2], t[3], ALU.add, [r_rt[2], r_rt[3]], [r_rq[b]])
            else:
                self.cp("act", rq[b][:, 0:512], ps[:, 0:512], [pres[0]], [r_rq[b]])
                self.cp("act", rq[b][:, 512:896], ps[:, 512:896], [pres[1]], [r_rq[b]])
                cb = T % 2
                self.cp("dve", cst[cb][:, 0:128], ps[:, C_AK:C_AK + 128], [pres[0]], [r_cst[cb]])
                self.cp("dve", cst[cb][:, 128:384], ps[:, C_DK:C_DK + 256], [pres[1]], [r_cst[cb]])
                self.cp("dve", cst[cb][:, 384:768], ps[:, C_AV:C_AV + 384], [pres[1], pres[2]], [r_cst[cb]])
                pj = (T - 32) // 2
                t0 = ((T - 32) % 2) * 128
                self.dma(STQ, self.out["nak"][pj, l, t0:t0 + 128, :], cst[cb][:, 0:128], reads=[r_cst[cb]])
                self.dma(STQ, self.out["ndk"][pj, l, t0:t0 + 128, :], cst[cb][:, 128:384], reads=[r_cst[cb]])
                self.dma(STQ, self.out["nav"][pj, l, t0:t0 + 128, :], cst[cb][:, 384:512], reads=[r_cst[cb]])
                self.dma(STQ, self.out["ndv"][pj, l, t0:t0 + 128, :], cst[cb][:, 512:768], reads=[r_cst[cb]])
            self.cp("act", vs_st[sb][:, tt_, 0:128], ps[:, C_AV:C_AV + 128], [pres[1]], [r_vs[sb]])
            self.cp("act", vs_st[sb][:, tt_, 128:384], ps[:, C_DV:C_DV + 256], [pres[2]], [r_vs[sb]])
            self.act(sg_st[sb][:, tt_, 0:256], ps[:, C_AG:C_AG + 256], AF.Silu, [pres[2]], [r_sg[sb]])
            self.act(sg_st[sb][:, tt_, 256:512], ps[:, C_DG:C_DG + 256], AF.Silu, [pres[3]], [r_sg[sb]])
            self.act(sg_st[sb][:, tt_, 512:768], ps[:, C_FG:C_FG + 256], AF.Silu, [pres[4]], [r_sg[sb]])
            self.cp("act", fu_st[sb][:, tt_, :], ps[:, C_FU:C_FU + 256], [pres[3]], [r_fu[sb]])
            pq = self.bank(6, BF16)
            for c in range(7):
                self.tr(pq[:, c * 128:(c + 1) * 128], rq[b][:, c * 128:(c + 1) * 128], [r_rq[b]], [pres[6]])
            self.cp("dve", qkt_st[sb][:, :, tt_ * 128:(tt_ + 1) * 128], pq[:, 0:896].rearrange("p (c t) -> p c t", t=128),
                    [pres[6]], [r_qkt[sb]])

        def superFM(st):
            sb = st % 2
            for j in range(8):
                bk = 7
                for k in range(8):
                    self.mm(self.bank(bk), win[:, k, NTM + j * 128: NTM + (j + 1) * 128], hT[sb][:, k, :], k == 0, k == 7,
                            [r_win, r_hT[sb]], [pres[bk]])
                if j < 6:
                    self.cp("act" if j % 2 == 0 else "dve", hut_st[sb][:, j, :], self.bank(bk), [pres[bk]], [r_hut[sb]])
                else:
                    self.act(sght_st[sb][:, j - 6, :], self.bank(bk), AF.Silu, [pres[bk]], [r_sght[sb]])
            t0 = st * 512
            self.dma(STQ, QKTv[:, :, t0:t0 + 512], qkt_st[sb], reads=[r_qkt[sb]])
            self.dma(STQ, S["VS"][t0:t0 + 512, :].rearrange("(a p) c -> p a c", p=128), vs_st[sb], reads=[r_vs[sb]])
            self.dma(STQ, S["SG"][t0:t0 + 512, :].rearrange("(a p) c -> p a c", p=128), sg_st[sb], reads=[r_sg[sb]])
            self.dma(STQ, S["FU"][t0:t0 + 512, :].rearrange("(a p) c -> p a c", p=128), fu_st[sb], reads=[r_fu[sb]])
            self.dma(STQ, HUTv[:, :, t0:t0 + 512], hut_st[sb], reads=[r_hut[sb]])
            self.dma(STQ, SGHTv[:, :, t0:t0 + 512], sght_st[sb], reads=[r_sght[sb]])

        stageA(0)
        stageB(0)
        for T in range(NT):
            if T + 1 < NT:
                stageA(T + 1)
            stageC(T)
            if T % 4 == 3:
                superFM(T // 4)
            if T + 1 < NT:
                stageB(T + 1)
        A.release()

    def half(self, l):
        return self.nlayers == DEPTH and l == DEPTH - 1

    def seqs(self):
        return [("s", 0, 32)] + [("p%d" % j, 32 + 2 * j, 2) for j in range(NPR)]

    def load_ctx_T(self, src, ncol, dstT, r_dst, ps_bank):
        A = self.A
        A.mark()
        raw = A.alloc([128, 4, ncol], F32)
        rawb = A.alloc([128, 4, ncol], BF16)
        r_raw = Res("ctxraw")
        r_rawb = Res("ctxrawb")
        self.dma("sp", raw, src.rearrange("(a p) c -> p a c", p=128), writes=[r_raw])
        self.cp("dve", rawb, raw, [r_raw], [r_rawb])
        pb = self.bank(ps_bank, BF16)
        for c in range(ncol // 128):
            for a in range(4):
                self.tr(pb[:, a * 128:(a + 1) * 128], rawb[:, a, c * 128:(c + 1) * 128], [r_rawb], [self.pres[ps_bank]])
            if ncol == 128:
                self.cp("dve", dstT, pb[:, 0:512], [self.pres[ps_bank]], [r_dst])
            else:
                self.cp("dve", dstT[:, c, :], pb[:, 0:512], [self.pres[ps_bank]], [r_dst])
        A.release()
        return raw

    def attnA(self, l):
        A, R, I, S = self.A, self.R, self.inp, self.S
        pres = self.pres
        A.mark()
        QT = A.alloc([128, 2, TOK], BF16)
        KT = A.alloc([128, TOK], BF16)
        KTc = A.alloc([128, 512], BF16)
        Vaug = A.alloc([128, NT + 4, 2, 65], BF16)
        r_q = Res("aQT"); r_k = Res("aKT"); r_kc = Res("aKTc"); r_v = Res("aV")
        QKTv = S["QKT"].rearrange("(c p) t -> p c t", p=128)
        self.dma("sp", QT, QKTv[:, 0:2, :], writes=[r_q])
        self.dma("sp", KT, S["QKT"][256:384, :], writes=[r_k])
        mprev = A.alloc([128, 128], BF16); mnext = A.alloc([128, 128], BF16)
        r_m = Res("amask")
        self.dma("sp", mprev, I["mprev"], writes=[r_m])
        R.op("sp", lambda e: e.dma_start(out=mnext, in_=I["mnext"]), writes=[r_m], dma=True, accum=True)
        esink = A.alloc([128, 4], F32)
        r_es = Res("esink")
        self.dma("sp", esink, I["attn_sink"][l:l + 1, :].broadcast_to([128, 4]), writes=[r_es])
        self.act(esink, esink, AF.Exp, [r_es], [r_es])
        A.mark()
        vraw = A.alloc([128, NT, 128], BF16)
        vcraw = A.alloc([128, 4, 128], F32)
        r_vr = Res("avraw")
        self.dma("sp", vraw, S["VS"][:, 0:128].rearrange("(a p) c -> p a c", p=128), writes=[r_vr])
        R.op("sp", lambda e: e.dma_start(out=vcraw, in_=I["cav"][l].rearrange("(a p) c -> p a c", p=128)), writes=[r_vr], dma=True, accum=True)
        R.op("pool", lambda e: e.memset(Vaug, 1.0), writes=[r_v])
        self.cp("dve", Vaug[:, 0:NT, :, 0:64], vraw.rearrange("p a (g d) -> p a g d", d=64), [r_vr], [r_v])
        self.cp("dve", Vaug[:, NT:NT + 4, :, 0:64], vcraw.rearrange("p a (g d) -> p a g d", d=64), [r_vr], [r_v])
        self.load_ctx_T(I["cak"][l], 128, KTc, r_kc, 7)
        R.barrier()
        A.release()
        NB = 2
        gate = [A.alloc([128, 256], BF16) for _ in range(NB)]
        r_g = [Res("agate%d" % i) for i in range(NB)]
        den = [A.alloc([128, 4], F32) for _ in range(NB)]
        tmpo = [A.alloc([128, 4, 64], F32) for _ in range(NB)]
        mixo = [A.alloc([128, 256], BF16) for _ in range(NB)]
        r_fin = [Res("afin%d" % i) for i in range(NB)]
        r_mx = [Res("amix%d" % i) for i in range(NB)]
        units = []
        for (nm, t0, ntl) in self.seqs():
            for qi in range(ntl):
                T = t0 + qi
                if nm == "s" and self.half(l) and qi >= ntl // 2:
                    continue
                if nm == "s":
                    chunks = []
                    if qi > 0:
                        chunks.append((T - 1, "prev"))
                    chunks.append((T, None))
                    if qi < ntl - 1:
                        chunks.append((T + 1, "next"))
                    chunks += [(NT + a, "ctx") for a in range(4)]
                else:
                    chunks = [(t0 + a, None) for a in range(ntl)]
                for h in range(4):
                    units.append((T, h, chunks))
        NPT = 3
        PT = [A.alloc([128, 7, 128], BF16) for _ in range(NPT)]
        r_pt = [Res("aPT%d" % i) for i in range(NPT)]

        def qk(u):
            T, h, chunks = units[u]
            qc = 0 if h in (0, 2) else 1
            pb0 = 0 if h < 2 else 64
            sbk = (0, 2, 6)[u % 3]
            sps = self.psum[:, sbk * 512:(sbk + 2) * 512]
            for ci, (kt, kind) in enumerate(chunks):
                if kind == "ctx":
                    lhsT = KTc[pb0:pb0 + 64, (kt - NT) * 128:(kt - NT + 1) * 128]
                    rk = r_kc
                else:
                    lhsT = KT[pb0:pb0 + 64, kt * 128:(kt + 1) * 128]
                    rk = r_k
                bk = sbk + (ci * 128) // 512
                self.mm(sps[:, ci * 128:(ci + 1) * 128], lhsT, QT[pb0:pb0 + 64, qc, T * 128:(T + 1) * 128], True, True,
                        [rk, r_q], [pres[bk]], tile_position=(pb0, 0))

        qk(0)
        qk(1)
        for u, (T, h, chunks) in enumerate(units):
            b = T % NB
            if h == 0:
                self.dma("sp", gate[b], S["SG"][T * 128:(T + 1) * 128, 0:256], writes=[r_g[b]])
            g = h // 2
            sbk = (0, 2, 6)[u % 3]
            pbuf = u % NPT
            sps = self.psum[:, sbk * 512:(sbk + 2) * 512]
            ob = 4 + (T % 2)
            O = self.bank(ob)[:, 0:260].rearrange("p (h d) -> p h d", d=65)
            nch = len(chunks)
            ptv = PT[pbuf][:, 0:nch, :]
            rb = [pres[sbk]] + ([pres[sbk + 1]] if nch > 4 else [])
            self.act(ptv, sps[:, 0:nch * 128].rearrange("p (c q) -> p c q", q=128), AF.Exp, rb, [r_pt[pbuf]], scale=0.125)
            for ci, (kt, kind) in enumerate(chunks):
                if kind == "prev":
                    self.tt("dve", PT[pbuf][:, ci, :], PT[pbuf][:, ci, :], mprev, ALU.mult, [r_pt[pbuf], r_m], [r_pt[pbuf]])
                elif kind == "next":
                    self.tt("dve", PT[pbuf][:, ci, :], PT[pbuf][:, ci, :], mnext, ALU.mult, [r_pt[pbuf], r_m], [r_pt[pbuf]])
            if u + 2 < len(units):
                qk(u + 2)
            for ci, (kt, kind) in enumerate(chunks):
                self.mm(O[:, h, :], PT[pbuf][:, ci, :], Vaug[:, kt, g, :], ci == 0, ci == nch - 1,
                        [r_pt[pbuf], r_v], [pres[ob]])
            if h == 3:
                self.tt("dve", den[b], O[:, :, 64], esink, ALU.add, [pres[ob], r_es], [r_fin[b]])
                self.recip(den[b], den[b], [r_fin[b]], [r_fin[b]])
                self.tt("dve", tmpo[b], O[:, :, 0:64], den[b].unsqueeze(2).broadcast_to([128, 4, 64]), ALU.mult, [pres[ob], r_fin[b]], [r_fin[b]])
                self.tt("dve", mixo[b], tmpo[b].rearrange("p h d -> p (h d)"), gate[b], ALU.mult, [r_fin[b], r_g[b]], [r_mx[b]])
                self.dma(STQ, S["MIX"][T * 128:(T + 1) * 128, 0:256], mixo[b], reads=[r_mx[b]])
        A.release()

    def attnB(self, l):
        A, R, I, S = self.A, self.R, self.inp, self.S
        pres = self.pres
        lam_init = 0.8 - 0.6 * math.exp(-0.3 * l)
        A.mark()
        QTm = [A.alloc([128, 2, TOK], BF16) for _ in range(4)]
        KT = A.alloc([128, 2, TOK], BF16)
        KTc = A.alloc([128, 2, 512], BF16)
        Vaug = A.alloc([128, NT + 4, 4, 128], BF16)
        r_q = Res("bQT"); r_k = Res("bKT"); r_kc = Res("bKTc"); r_v = Res("bV")
        QKTv = S["QKT"].rearrange("(c p) t -> p c t", p=128)
        for h in range(4):
            self.memset("dve" if h % 2 == 0 else "pool", QTm[h], 0.0, [r_q] if h == 0 else [])
        R.barrier()
        for h in range(4):
            self.dma("sp", QTm[h][32 * h:32 * h + 32, :, :], QKTv[32 * h:32 * h + 32, 3:5, :], writes=[r_q], accum=True)
        self.dma("sp", KT, QKTv[:, 5:7, :], writes=[r_k])
        lam = A.alloc([128, 4, 32], F32)
        lp = A.alloc([128, 2, 32], F32)
        ls = A.alloc([128, 2], F32)
        nlam = A.alloc([128, 1], F32)
        r_lam = Res("lam")
        self.dma("sp", lam, I["diff_lambda"][l:l + 1, :].broadcast_to([128, 128]).rearrange("p (a b) -> p a b", b=32), writes=[r_lam])
        lam4 = lam.rearrange("p (a two) d -> p a two d", two=2)
        self.tt("dve", lp, lam4[:, :, 0, :], lam4[:, :, 1, :], ALU.mult, [r_lam], [r_lam])
        R.op("dve", lambda e: e.tensor_reduce(out=ls, in_=lp, axis=AX.X, op=ALU.add), reads=[r_lam], writes=[r_lam])
        self.act(ls, ls, AF.Exp, [r_lam], [r_lam])
        self.tt("dve", nlam, ls[:, 1:2], ls[:, 0:1], ALU.subtract, [r_lam], [r_lam])
        self.ts("dve", nlam, nlam, -lam_init, None, ALU.add, ALU.bypass, [r_lam], [r_lam])
        sl = A.alloc([128, 64], F32)
        r_sl = Res("subln")
        self.dma("sp", sl, I["diff_subln"][l:l + 1, :].broadcast_to([128, 64]), writes=[r_sl])
        self.ts("dve", sl, sl, 1.0 - lam_init, None, ALU.mult, ALU.bypass, [r_sl], [r_sl])
        A.mark()
        vraw = A.alloc([128, NT, 256], BF16)
        vcraw = A.alloc([128, 4, 256], F32)
        r_vr = Res("bvraw")
        self.dma("sp", vraw, S["VS"][:, 128:384].rearrange("(a p) c -> p a c", p=128), writes=[r_vr])
        R.op("sp", lambda e: e.dma_start(out=vcraw, in_=I["cdv"][l].rearrange("(a p) c -> p a c", p=128)), writes=[r_vr], dma=True, accum=True)
        self.memset("pool", Vaug, 0.0, [r_v])
        self.memset("pool", Vaug[:, :, :, 64:65], 1.0, [r_v])
        self.cp("dve", Vaug[:, 0:NT, :, 0:64], vraw.rearrange("p a (g d) -> p a g d", d=64), [r_vr], [r_v])
        self.cp("dve", Vaug[:, NT:NT + 4, :, 0:64], vcraw.rearrange("p a (g d) -> p a g d", d=64), [r_vr], [r_v])
        self.load_ctx_T(I["cdk"][l], 256, KTc, r_kc, 7)
        R.barrier()
        A.release()
        PT = [A.alloc([128, 2, 512], BF16) for _ in range(3)]
        r_pt = [Res("bPT%d" % i) for i in range(3)]
        identf = A.alloc([128, 128], F32)
        r_idf = Res("identf")
        self.dma("sp", identf, I["identf"], writes=[r_idf])
        OT = [A.alloc([65, 4, 512], F32) for _ in range(2)]
        r_ot = [Res("bOT%d" % i) for i in range(2)]
        NB = 2
        gate = [A.alloc([128, 256], BF16) for _ in range(NB)]
        r_g = [Res("bgate%d" % i) for i in range(NB)]
        rd = [A.alloc([128, 2, 4], F32) for _ in range(NB)]
        ta = [A.alloc([128, 4, 64], F32) for _ in range(NB)]
        tb = [A.alloc([128, 4, 64], F32) for _ in range(NB)]
        ssq = [A.alloc([128, 4], F32) for _ in range(NB)]
        mixo = [A.alloc([128, 256], BF16) for _ in range(NB)]
        r_f = [Res("bfin%d" % i) for i in range(NB)]
        r_mx = [Res("bmix%d" % i) for i in range(NB)]
        rounds = []
        for (nm, t0, ntl) in self.seqs():
            W = 512 if nm == "s" else 256
            ktiles = list(range(t0, t0 + ntl)) + ([NT + a for a in range(4)] if nm == "s" else [])
            nqc = ntl * 128 // W
            if nm == "s" and self.half(l):
                nqc //= 2
            for qc in range(nqc):
                q0 = t0 * 128 + qc * W
                for m in range(2):
                    for hp in range(2):
                        rounds.append((q0, W, m, hp, ktiles))
        stages = []
        for ri, (q0, W, m, hp, ktiles) in enumerate(rounds):
            for ki, kt in enumerate(ktiles):
                stages.append((ri, ki, kt))

        def qk(si):
            ri, ki, kt = stages[si]
            q0, W, m, hp, ktiles = rounds[ri]
            sb = (si % 2) * 2
            for hh in range(2):
                h = hp * 2 + hh
                if kt >= NT:
                    lhsT = KTc[:, m, (kt - NT) * 128:(kt - NT + 1) * 128]
                    rk = r_kc
                else:
                    lhsT = KT[:, m, kt * 128:(kt + 1) * 128]
                    rk = r_k
                self.mm(self.bank(sb + hh)[:, 0:W], lhsT, QTm[h][:, m, q0:q0 + W], True, True,
                        [rk, r_q], [pres[sb + hh]])

        def fin(qs, q0, ob):
            T = q0 // 128 + qs
            b = T % NB

            self.dma("sp", gate[b], S["SG"][T * 128:(T + 1) * 128, 256:512], writes=[r_g[b]])
            ov = []
            for mm_ in range(2):
                tbk = ob + mm_
                for h in range(4):
                    self.tr(self.bank(tbk)[:, h * 65:(h + 1) * 65], OT[mm_][:, h, qs * 128:(qs + 1) * 128], [r_ot[mm_], r_idf], [pres[tbk]], ident=identf)
                ov.append(self.bank(tbk)[:, 0:260].rearrange("p (h d) -> p h d", d=65))
            o0, o1 = ov
            self.recip(rd[b][:, 0, :], o0[:, :, 64], [pres[ob]], [r_f[b]])
            self.recip(rd[b][:, 1, :], o1[:, :, 64], [pres[ob + 1]], [r_f[b]])
            self.ts("dve", rd[b][:, 1, :], rd[b][:, 1, :], nlam, None, ALU.mult, ALU.bypass, [r_f[b], r_lam], [r_f[b]])
            self.tt("dve", ta[b], o0[:, :, 0:64], rd[b][:, 0, :].unsqueeze(2).broadcast_to([128, 4, 64]), ALU.mult, [pres[ob], r_f[b]], [r_f[b]])
            self.tt("dve", tb[b], o1[:, :, 0:64], rd[b][:, 1, :].unsqueeze(2).broadcast_to([128, 4, 64]), ALU.mult, [pres[ob + 1], r_f[b]], [r_f[b]])
            self.tt("dve", ta[b], ta[b], tb[b], ALU.add, [r_f[b]], [r_f[b]])
            self.tt("dve", tb[b], ta[b], ta[b], ALU.mult, [r_f[b]], [r_f[b]])
            R.op("dve", lambda e, b=b: e.tensor_reduce(out=ssq[b], in_=tb[b], axis=AX.X, op=ALU.add), reads=[r_f[b]], writes=[r_f[b]])
            self.ts("dve", ssq[b], ssq[b], 1.0 / 64, EPS, ALU.mult, ALU.add, [r_f[b]], [r_f[b]])
            self.tt("pool", ssq[b], ssq[b], self.neghalf.broadcast_to([128, 4]), ALU.pow, [r_f[b], self.r_nh], [r_f[b]])
            self.tt("dve", ta[b], ta[b], ssq[b].unsqueeze(2).broadcast_to([128, 4, 64]), ALU.mult, [r_f[b]], [r_f[b]])
            self.tt("dve", ta[b], ta[b], sl.unsqueeze(1).broadcast_to([128, 4, 64]), ALU.mult, [r_f[b], r_sl], [r_f[b]])
            self.tt("dve", mixo[b], ta[b].rearrange("p h d -> p (h d)"), gate[b], ALU.mult, [r_f[b], r_g[b]], [r_mx[b]])
            self.dma(STQ, S["MIX"][T * 128:(T + 1) * 128, 256:512], mixo[b], reads=[r_mx[b]])


        pending = []

        qk(0)
        qk(1)
        for si, (ri, ki, kt) in enumerate(stages):
            q0, W, m, hp, ktiles = rounds[ri]
            nq = W // 128
            if len(ktiles) < 16:
                while pending:
                    pending.pop(0)[1]()
            elif pending and si >= pending[0][0]:
                pending.pop(0)[1]()
            sb = (si % 2) * 2
            pbuf = si % 3
            ob = 4 + (ri % 2) * 2
            src = self.psum[:, sb * 512:(sb + 2) * 512].rearrange("p (h q) -> p h q", q=512)[:, :, 0:W]
            self.act(PT[pbuf][:, :, 0:W], src, AF.Exp, [pres[sb], pres[sb + 1]], [r_pt[pbuf]], scale=32.0 ** -0.5)
            if si + 2 < len(stages):
                qk(si + 2)
            for hh in range(2):
                h = hp * 2 + hh
                self.mm(self.bank(ob + hh)[:, 0:W], Vaug[:, kt, h, :], PT[pbuf][:, hh, 0:W], ki == 0, ki == len(ktiles) - 1,
                        [r_pt[pbuf], r_v], [pres[ob + hh]])
            if ki != len(ktiles) - 1:
                continue
            srcO = self.psum[0:65, ob * 512:(ob + 2) * 512].rearrange("p (h q) -> p h q", q=512)[:, :, 0:W]
            self.cp("dve", OT[m][:, 2 * hp:2 * hp + 2, 0:W], srcO, [pres[ob], pres[ob + 1]], [r_ot[m]])
            if not (m == 1 and hp == 1):
                continue
            defer = (len(ktiles) >= 16) and os.environ.get("BDEFER", "1") == "1"
            for qs in range(nq):
                if defer:
                    pending.append((si + 6 + 5 * qs, (lambda qs=qs, q0=q0, ob=ob: fin(qs, q0, ob))))
                else:
                    fin(qs, q0, ob)
        while pending:
            pending.pop(0)[1]()
        A.release()

    def fnet(self, l):
        A, R, I, S = self.A, self.R, self.inp, self.S
        pres = self.pres
        A.mark()
        fw = A.alloc([128, 2, 256], F32)
        bdc = A.alloc([128, 128], F32); bds = A.alloc([128, 128], F32)
        Wt = [A.alloc([128, 2, 256], BF16) for _ in range(2)]
        fnb = A.alloc([128, 256], F32)
        r_c = Res("fconst"); r_W = Res("fW")
        self.dma("sp", fw, I["fn_w"][l].rearrange("(a p) c -> p a c", p=128), writes=[r_c])
        R.op("sp", lambda e: e.dma_start(out=bdc, in_=I["bdc"]), writes=[r_c], dma=True, accum=True)
        R.op("sp", lambda e: e.dma_start(out=bds, in_=I["bds"]), writes=[r_c], dma=True, accum=True)
        R.op("sp", lambda e: e.dma_start(out=fnb, in_=I["fn_b"][l:l + 1, :].broadcast_to([128, 256])), writes=[r_c], dma=True, accum=True)
        for ti, bd in enumerate((bdc, bds)):
            for ch in range(2):
                bk = ti * 2 + ch
                self.mm(self.bank(bk)[:, 0:256], bd, fw[:, ch, :], True, True, [r_c], [pres[bk]])
                self.cp("dve", Wt[ti][:, ch, :], self.bank(bk)[:, 0:256], [pres[bk]], [r_W])
        KW = 256
        NB = 3
        tab = [[A.alloc([128, 32, KW], BF16) for _ in range(2)] for _ in range(NB)]
        r_tab = [Res("ftab%d" % i) for i in range(NB)]
        AT = [A.alloc([128, 2, 2, KW], BF16) for _ in range(NB)]
        r_at = [Res("fAT%d" % i) for i in range(NB)]
        gate = [A.alloc([128, 256], BF16) for _ in range(NB)]
        r_g = [Res("fgate%d" % i) for i in range(NB)]
        tmp = [A.alloc([128, 256], F32) for _ in range(NB)]
        mixo = [A.alloc([128, 256], BF16) for _ in range(NB)]
        r_t = [Res("ftmp%d" % i) for i in range(NB)]
        r_mx = [Res("fmix%d" % i) for i in range(NB)]
        u = A.alloc([128, 32, 256], BF16)
        r_u = Res("fu")
        kcc = 0
        for (nm, t0, ntl) in self.seqs():
            n = ntl * 128
            tag = "s" if nm == "s" else "p"
            scale = 1.0 / math.sqrt(n * 64.0)
            self.dma("sp", u[:, 0:ntl, :], S["FU"][t0 * 128:(t0 + ntl) * 128, :].rearrange("(a p) c -> p a c", p=128), writes=[r_u])
            nkc = n // KW
            if nm == "s" and self.half(l):
                nkc //= 2
            for kc in range(nkc):
                tb_ = kcc % NB
                kcc += 1
                self.dma("sp", tab[tb_][0][:, 0:ntl, :], I["fc" + tag][kc], writes=[r_tab[tb_]])
                R.op("sp", lambda e, tb_=tb_, kc=kc, tag=tag, ntl=ntl: e.dma_start(out=tab[tb_][1][:, 0:ntl, :], in_=I["fs" + tag][kc]),
                     writes=[r_tab[tb_]], dma=True, accum=True)
                for ti in range(2):
                    for ch in range(2):
                        bk = ti
                        for tc in range(ntl):
                            self.mm(self.bank(bk)[:, ch * 256:(ch + 1) * 256], u[:, tc, ch * 128:(ch + 1) * 128], tab[tb_][ti][:, tc, :],
                                    tc == 0, tc == ntl - 1, [r_u, r_tab[tb_]], [pres[bk]])
                    self.cp("act", AT[tb_][:, ti, :, :], self.bank(ti).rearrange("p (c k) -> p c k", k=KW), [pres[ti]], [r_at[tb_]])
                for ks in range(KW // 128):
                    T = t0 + (kc * KW) // 128 + ks
                    b = T % NB
                    ob = 2 + (T % 2)
                    self.dma("sp", gate[b], S["SG"][T * 128:(T + 1) * 128, 512:768], writes=[r_g[b]])
                    i = 0
                    for ti in range(2):
                        for ch in range(2):
                            self.mm(self.bank(ob)[:, 0:256], AT[tb_][:, ti, ch, ks * 128:(ks + 1) * 128], Wt[ti][:, ch, :], i == 0, i == 3,
                                    [r_at[tb_], r_W], [pres[ob]])
                            i += 1
                    self.stt(tmp[b], self.bank(ob)[:, 0:256], scale, fnb, ALU.mult, ALU.add, [pres[ob], r_c], [r_t[b]])
                    self.tt("dve", mixo[b], tmp[b], gate[b], ALU.mult, [r_t[b], r_g[b]], [r_mx[b]])
                    self.dma(STQ, S["MIX"][T * 128:(T + 1) * 128, 768:1024], mixo[b], reads=[r_mx[b]])
        A.release()

    def sin3(self, out, arg, tmp, reads, r_tmp, writes):
        s_, q_ = tmp
        self.act(s_, arg, AF.Sin, reads, [r_tmp], scale=1.0 / 3.0)
        self.tt("dve", q_, s_, s_, ALU.mult, [r_tmp], [r_tmp])
        self.ts("dve", q_, q_, -4.0, 3.0, ALU.mult, ALU.add, [r_tmp], [r_tmp])
        self.tt("dve", out, q_, s_, ALU.mult, [r_tmp], writes)

    def hyena(self, l):
        A, R, I, S = self.A, self.R, self.inp, self.S
        pres = self.pres
        A.mark()
        cw = A.alloc([128, 6, 4], F32)
        skip = A.alloc([128, 2], F32)
        w1 = A.alloc([33, 64], F32); w2 = A.alloc([64, 64], F32); w3 = A.alloc([64, 512], F32)
        b1 = A.alloc([64, 1], F32); b2 = A.alloc([64, 1], F32); fr = A.alloc([64, 1], F32)
        ones = A.alloc([128, 128], F32)
        r_p = Res("hparams")
        first = True
        for dst, src in ((cw, I["hcw"][l]), (skip, I["hskip"][l]), (w1, I["hw1"][l]), (w2, I["hw2"][l]), (w3, I["hw3"][l]),
                         (b1, I["hb1"][l]), (b2, I["hb2"][l]), (fr, I["hfreq"][l]), (ones, I["ones"])):
            self.dma("sp", dst, src, writes=[r_p], accum=not first)
            first = False
        R.barrier()
        groups = [("s", 0, 32, 1), ("p", 32, 2, NPR)]
        for (tag, t0, ntl, nseq) in groups:
            n = ntl * 128
            N = 2 * n
            ntt = ntl * nseq
            A.mark()
            Ztok = A.alloc([128, ntt, 256], BF16)
            M1 = A.alloc([128, ntt, 256], BF16)
            M2 = A.alloc([128, ntt, 256], BF16)
            r_z = Res("hZ"); r_m1 = Res("hM1"); r_m2 = Res("hM2")
            A.mark()
            ntok = ntt * 128
            c0 = t0 * 128
            HUTv = S["HUT"].rearrange("(c p) t -> p c t", p=128)
            SGHTv = S["SGHT"].rearrange("(c p) t -> p c t", p=128)
            hu = A.alloc([128, 3, ntok], BF16)
            sgh = A.alloc([128, ntok], BF16)
            uc = [A.alloc([128, ntok], F32) for _ in range(3)]
            zT = A.alloc([128, ntok], BF16); m1T = A.alloc([128, ntok], BF16); m2T = A.alloc([128, ntok], BF16)
            r_hu = Res("hhu"); r_uc = [Res("huc%d" % i) for i in range(3)]; r_fm = Res("hfm")
            for ch in range(2):
                for j in range(3):
                    self.dma("sp", hu[:, j, :], HUTv[:, 2 * j + ch, c0:c0 + ntok], writes=[r_hu], accum=(j > 0))
                self.dma("sp", sgh, SGHTv[:, ch, c0:c0 + ntok], writes=[r_hu], accum=True)
                for j in range(3):
                    cj = 2 * j + ch
                    self.act(uc[j], hu[:, j, :], AF.Identity, [r_hu, r_p], [r_uc[j]], scale=cw[:, cj, 1:2], bias=cw[:, cj, 3:4])
                    for sq in range(nseq):
                        a = sq * n
                        self.stt(uc[j][:, a + 1:a + n], hu[:, j, a:a + n - 1], cw[:, cj, 0:1], uc[j][:, a + 1:a + n], ALU.mult, ALU.add,
                                 [r_hu, r_p, r_uc[j]], [r_uc[j]])
                        self.stt(uc[j][:, a:a + n - 1], hu[:, j, a + 1:a + n], cw[:, cj, 2:3], uc[j][:, a:a + n - 1], ALU.mult, ALU.add,
                                 [r_hu, r_p, r_uc[j]], [r_uc[j]])
                self.tt("dve", zT, uc[1], uc[2], ALU.mult, [r_uc[1], r_uc[2]], [r_fm])
                self.tt("pool", m1T, uc[0], sgh, ALU.mult, [r_uc[0], r_hu], [r_fm])
                self.stt(m2T, zT, skip[:, ch:ch + 1], m1T, ALU.mult, ALU.mult, [r_fm, r_p], [r_fm])
                for ti in range(ntt):
                    bk = ti % 2
                    pb = self.bank(bk, BF16)
                    for k, srcT in enumerate((zT, m1T, m2T)):
                        self.tr(pb[:, k * 128:(k + 1) * 128], srcT[:, ti * 128:(ti + 1) * 128], [r_fm], [pres[bk]])
                    self.cp("act", Ztok[:, ti, ch * 128:(ch + 1) * 128], pb[:, 0:128], [pres[bk]], [r_z])
                    self.cp("act", M1[:, ti, ch * 128:(ch + 1) * 128], pb[:, 128:256], [pres[bk]], [r_m1])
                    self.cp("dve", M2[:, ti, ch * 128:(ch + 1) * 128], pb[:, 256:384], [pres[bk]], [r_m2])
            R.barrier()
            A.release()
            ka = A.alloc([128, ntl, 256], BF16)
            kb = A.alloc([128, ntl, 256], BF16)
            r_ka = Res("hka")
            h0bc = A.alloc([128, 256], F32)
            r_h0 = Res("hh0")
            rot = A.alloc([128, ntl, 3], F32)
            r_rot = Res("hrot")
            self.dma("sp", rot, I["rot" + tag], writes=[r_rot])
            A.mark()
            hraw = A.alloc([128, ntl, 512], BF16)
            r_hr = Res("hraw")
            acc = A.alloc([128, 512], F32)
            r_acc = Res("hacc")
            self.memset("pool", acc, 0.0, [r_acc])
            CW = min(512, n)
            NBF = 2
            ft = [A.alloc([33, CW], F32) for _ in range(NBF)]
            r_ft = [Res("hft%d" % i) for i in range(NBF)]
            a1 = A.alloc([64, CW], F32); h1 = A.alloc([64, CW], F32); a2 = A.alloc([64, CW], F32); h2 = A.alloc([64, CW], F32)
            sc = [A.alloc([64, CW], F32) for _ in range(2)]
            r_a = Res("ha"); r_h1 = Res("hh1"); r_h2 = Res("hh2"); r_sc = Res("hsc")
            dec = [A.alloc([128, 256], F32) for _ in range(NBF)]
            r_dec = [Res("hdec%d" % i) for i in range(NBF)]
            ab = A.alloc([128, 512], F32)
            r_ab = Res("hab")
            fb1 = A.alloc([64, 1], F32); fb2 = A.alloc([64, 1], F32)
            r_fb = Res("hfb")
            self.tt("dve", fb1, b1, fr, ALU.mult, [r_p], [r_fb])
            self.tt("dve", fb2, b2, fr, ALU.mult, [r_p], [r_fb])
            for cc in range(n // CW):
                fb = cc % NBF
                self.dma("sp", ft[fb], I["feat" + tag][:, cc * CW:(cc + 1) * CW], writes=[r_ft[fb]])
                self.mm(self.bank(2)[0:64, 0:CW], w1, ft[fb], True, True, [r_p, r_ft[fb]], [pres[2]])
                self.ts("dve", a1, self.bank(2)[0:64, 0:CW], fr, fb1, ALU.mult, ALU.add, [pres[2], r_p, r_fb], [r_a])
                self.sin3(h1, a1, sc, [r_a], r_sc, [r_h1])
                self.mm(self.bank(3)[0:64, 0:CW], w2, h1, True, True, [r_p, r_h1], [pres[3]])
                self.ts("dve", a2, self.bank(3)[0:64, 0:CW], fr, fb2, ALU.mult, ALU.add, [pres[3], r_p, r_fb], [r_a])
                self.sin3(h2, a2, sc, [r_a], r_sc, [r_h2])
                for tt_ in range(CW // 128):
                    ti = cc * (CW // 128) + tt_
                    bk = 4 + ti % 2
                    db = ti % NBF
                    self.dma("sp", dec[db], I["dec" + tag][ti * 128:(ti + 1) * 128, :], writes=[r_dec[db]])
                    self.mm(self.bank(bk), h2[:, tt_ * 128:(tt_ + 1) * 128], w3, True, True, [r_h2, r_p], [pres[bk]])
                    self.tt("dve", hraw[:, ti, :].rearrange("p (d c) -> p d c", c=256), self.bank(bk).rearrange("p (d c) -> p d c", c=256),
                            dec[db].unsqueeze(1).broadcast_to([128, 2, 256]), ALU.mult, [pres[bk], r_dec[db]], [r_hr])
                    self.act(ab, hraw[:, ti, :], AF.Abs, [r_hr], [r_ab])
                    self.tt("pool", acc, acc, ab, ALU.add, [r_ab, r_acc], [r_acc])
            rn = A.alloc([128, 256], F32)
            r_rn = Res("hrn")
            self.mm(self.bank(6), ones, acc, True, True, [r_p, r_acc], [pres[6]])
            self.cp("dve", ab, self.bank(6), [pres[6]], [r_ab])
            self.tt("dve", rn, ab[:, 0:256], ab[:, 256:512], ALU.add, [r_ab], [r_rn])
            self.ts("dve", rn, rn, EPS, None, ALU.add, ALU.bypass, [r_rn], [r_rn])
            self.recip(rn, rn, [r_rn], [r_rn])
            hb_st = [A.alloc([128, 256], F32) for _ in range(2)]
            r_hbst = [Res("hbst%d" % i) for i in range(2)]
            hbs = [A.alloc([128, 256], F32) for _ in range(2)]
            r_hbs = [Res("hbs%d" % i) for i in range(2)]
            hfn = [A.alloc([128, 256], F32) for _ in range(2)]
            r_hfn = [Res("hfn%d" % i) for i in range(2)]
            for ti in range(ntl):
                b = ti % 2
                self.tt("dve", hb_st[b], hraw[:, ti, 256:512], rn, ALU.mult, [r_hr, r_rn], [r_hbst[b]])
                self.dma(STQ, S["HB"][ti * 128:(ti + 1) * 128, :], hb_st[b], reads=[r_hbst[b]])
            R.barrier()
            for ti in range(ntl):
                b = ti % 2
                if ti == ntl - 1:
                    self.memset("pool", hbs[b], 0.0, [r_hbs[b]])
                    self.dma("sp", hbs[b][0:127, :], S["HB"][ti * 128 + 1:(ti + 1) * 128, :], writes=[r_hbs[b]])
                else:
                    self.dma("sp", hbs[b], S["HB"][ti * 128 + 1:(ti + 1) * 128 + 1, :], writes=[r_hbs[b]])
                self.tt("dve", hfn[b], hraw[:, ti, 0:256], rn, ALU.mult, [r_hr, r_rn], [r_hfn[b]])
                if ti == 0:
                    hb0 = A.alloc([1, 256], F32)
                    h0row = A.alloc([1, 256], F32)
                    hfl = A.alloc([1, 2], F32)
                    r_h0r = Res("hh0row")
                    self.dma("sp", hfl, I["hflag"], writes=[r_h0r])
                    self.dma("sp", hb0, S["HB"][0:1, :], writes=[r_h0r], accum=True)
                    self.ts("dve", h0row, hfn[b][0:1, :], hfl[:, 0:1], None, ALU.mult, ALU.bypass, [r_hfn[b], r_h0r], [r_h0r])
                    self.stt(h0row, hb0, hfl[:, 1:2], h0row, ALU.mult, ALU.add, [r_h0r], [r_h0r])
                    self.mm(self.bank(7)[:, 0:256], ones[0:1, :], h0row, True, True, [r_p, r_h0r], [pres[7]])
                    self.cp("dve", h0bc, self.bank(7)[:, 0:256], [pres[7]], [r_h0])
                    self.memset("dve", hfn[b][0:1, :], 0.0, [r_hfn[b]])
                self.tt("dve", ka[:, ti, :], hfn[b], hbs[b], ALU.add, [r_hfn[b], r_hbs[b]], [r_ka])
                self.tt("pool", kb[:, ti, :], hfn[b], hbs[b], ALU.subtract, [r_hfn[b], r_hbs[b]], [r_ka])
            R.barrier()
            A.release()
            A.mark()
            Pr = A.alloc([128, ntl, nseq * 256], BF16)
            nPi = A.alloc([128, ntl, nseq * 256], BF16)
            r_P = Res("hP")
            NBP = 3
            pan = [[A.alloc([128, ntl, 128], BF16) for _ in range(2)] for _ in range(NBP)]
            r_pan = [Res("hpan%d" % i) for i in range(NBP)]
            Kr = A.alloc([128, 256], F32); Ki = A.alloc([128, 256], F32); t1 = A.alloc([128, 256], F32); t2 = A.alloc([128, 256], F32)
            r_K = Res("hK"); r_t = Res("ht")
            u1 = A.alloc([128, 256], F32); u2 = A.alloc([128, 256], F32)
            for kt in range(ntl):
                pb_ = kt % NBP
                self.dma("sp", pan[pb_][0], I["hc" + tag][kt], writes=[r_pan[pb_]])
                self.dma("sp", pan[pb_][1], I["hs" + tag][kt], writes=[r_pan[pb_]], accum=True)
                for ti_, rhs in enumerate((ka, kb)):
                    for tc in range(ntl):
                        self.mm(self.bank(0)[:, ti_ * 256:(ti_ + 1) * 256], pan[pb_][ti_][:, tc, :], rhs[:, tc, :], tc == 0, tc == ntl - 1,
                                [r_pan[pb_], r_ka], [pres[0]])
                Kc = self.bank(0)[:, 0:256]
                Ks = self.bank(0)[:, 256:512]
                rc = rot[:, kt, 0:1]; rs = rot[:, kt, 1:2]; nrc = rot[:, kt, 2:3]
                self.ts("dve", t1, Kc, rc, None, ALU.mult, ALU.bypass, [pres[0], r_rot], [r_t])
                self.stt(Kr, Ks, rs, t1, ALU.mult, ALU.add, [pres[0], r_rot, r_t], [r_K])
                self.ts("dve", t2, Kc, rs, None, ALU.mult, ALU.bypass, [pres[0], r_rot], [r_t])
                self.stt(Ki, Ks, nrc, t2, ALU.mult, ALU.add, [pres[0], r_rot, r_t], [r_K])
                for sq in range(nseq):
                    bk = 1 + (kt * nseq + sq) % 2
                    for ti_ in range(2):
                        for tc in range(ntl):
                            self.mm(self.bank(bk)[:, ti_ * 256:(ti_ + 1) * 256], pan[pb_][ti_][:, tc, :], Ztok[:, sq * ntl + tc, :], tc == 0, tc == ntl - 1,
                                    [r_pan[pb_], r_z], [pres[bk]])
                    Zc = self.bank(bk)[:, 0:256]
                    Zs = self.bank(bk)[:, 256:512]
                    osl = slice(sq * 256, (sq + 1) * 256)
                    self.tt("dve", u1, Zc, Kr, ALU.mult, [pres[bk], r_K], [r_t])
                    self.tt("dve", u2, Zs, Ki, ALU.mult, [pres[bk], r_K], [r_t])
                    self.tt("pool", Pr[:, kt, osl], u1, u2, ALU.add, [r_t], [r_P])
                    self.tt("dve", t1, Zs, Kr, ALU.mult, [pres[bk], r_K], [r_t])
                    self.tt("dve", t2, Zc, Ki, ALU.mult, [pres[bk], r_K], [r_t])
                    self.tt("pool", nPi[:, kt, osl], t1, t2, ALU.subtract, [r_t], [r_P])
            R.barrier()
            NBO = 2
            tmpy = [A.alloc([128, 256], F32) for _ in range(NBO)]
            mixo = [A.alloc([128, 256], BF16) for _ in range(NBO)]
            r_ty = [Res("hty%d" % i) for i in range(NBO)]
            r_mx = [Res("hmx%d" % i) for i in range(NBO)]
            oc = 0
            ntl_out = ntl // 2 if (tag == "s" and self.half(l)) else ntl
            hz = [A.alloc([128, 256], F32) for _ in range(NBO)]
            r_hz = [Res("hhz%d" % i) for i in range(NBO)]
            for tt_ in range(ntl_out):
                pb_ = tt_ % NBP
                self.dma("sp", pan[pb_][0], I["hc" + tag][tt_], writes=[r_pan[pb_]])
                self.dma("sp", pan[pb_][1], I["hs" + tag][tt_], writes=[r_pan[pb_]], accum=True)
                for sq in range(nseq):
                    bk = 4 + oc % 2
                    b = oc % NBO
                    oc += 1
                    osl = slice(sq * 256, (sq + 1) * 256)
                    i = 0
                    for ti_, src in enumerate((Pr, nPi)):
                        for kt in range(ntl):
                            self.mm(self.bank(bk)[:, 0:256], pan[pb_][ti_][:, kt, :], src[:, kt, osl], i == 0, i == 2 * ntl - 1,
                                    [r_pan[pb_], r_P], [pres[bk]])
                            i += 1
                    gt = sq * ntl + tt_
                    T = t0 + gt
                    self.tt("pool", hz[b], Ztok[:, gt, :], h0bc, ALU.mult, [r_z, r_h0], [r_hz[b]])
                    self.stt(tmpy[b], self.bank(bk)[:, 0:256], 2.0 / N, hz[b], ALU.mult, ALU.add, [pres[bk], r_hz[b]], [r_ty[b]])
                    self.tt("dve", tmpy[b], tmpy[b], M1[:, gt, :], ALU.mult, [r_ty[b], r_m1], [r_ty[b]])
                    self.tt("dve", mixo[b], tmpy[b], M2[:, gt, :], ALU.add, [r_ty[b], r_m2], [r_mx[b]])
                    self.dma(STQ, S["MIX"][T * 128:(T + 1) * 128, 512:768], mixo[b], reads=[r_mx[b]])
            R.barrier()
            A.release()
            A.release()
        A.release()

    def exit(self, l):
        A, R, I, S = self.A, self.R, self.inp, self.S
        A.mark()
        wout = A.alloc([128, 8, D], BF16)
        r_w = Res("wout")
        src = I["w_out"][l].rearrange("(k p) n -> p k n", p=128)
        for k in range(8):
            R.op("pool", lambda e, k=k: e.dma_start(out=wout[:, k, :], in_=src[:, k, :]), writes=[r_w], dma=True, accum=True)
        NB = 2
        mt = [A.alloc([128, D], BF16) for _ in range(NB)]
        r_mt = [Res("mt%d" % i) for i in range(NB)]
        mT = [A.alloc([128, 8, 128], BF16) for _ in range(NB)]
        r_mT = [Res("mT%d" % i) for i in range(NB)]
        xt = [A.alloc([128, D], F32) for _ in range(NB)]
        r_xt = [Res("ext%d" % i) for i in range(NB)]
        junk = A.alloc([128, D], BF16)
        r_junk = Res("ejunk")
        ss = [A.alloc([128, 1], F32) for _ in range(NB)]
        rstd = [A.alloc([128, 1], F32) for _ in range(NB)]
        r_ss = [Res("ess%d" % i) for i in range(NB)]
        tmp = [A.alloc([128, D], F32) for _ in range(NB)]
        r_tmp = [Res("etmp%d" % i) for i in range(NB)]
        xn = [A.alloc([128, D], F32) for _ in range(NB)]
        r_xn = [Res("xn%d" % i) for i in range(NB)]
        pres = self.pres
        tiles = list(range(NT))
        if self.half(l):
            tiles = list(range(16)) + list(range(32, NT))
        for T in tiles:
            cond = 0 if T < 32 else 1
            b = T % NB
            self.dma("sp", mt[b], S["MIX"][T * 128:(T + 1) * 128, :], writes=[r_mt[b]])
            self.dma("sp", xt[b], self.xsrc(l, T), writes=[r_xt[b]])
            pbk = 4 + (T % 2)
            pb = self.bank(pbk, BF16)
            for k in range(8):
                self.tr(pb[:, k * 128:(k + 1) * 128], mt[b][:, k * 128:(k + 1) * 128], [r_mt[b]], [pres[pbk]])
            self.cp("act", mT[b], pb.rearrange("p (k t) -> p k t", t=128), [pres[pbk]], [r_mT[b]])
            ob = (T % 2) * 2
            for c in range(2):
                for k in range(8):
                    self.mm(self.bank(ob + c), mT[b][:, k, :], wout[:, k, c * 512:(c + 1) * 512], k == 0, k == 7,
                            [r_mT[b], r_w], [pres[ob + c]])
            o = self.psum[:, ob * 512:(ob + 2) * 512]
            R.op("act", lambda e, b=b, o=o: e.activation(out=junk, in_=o, func=AF.Square, accum_out=ss[b]),
                 reads=[pres[ob], pres[ob + 1]], writes=[r_junk, r_ss[b]])
            self.ts("dve", ss[b], ss[b], 1.0 / D, EPS, ALU.mult, ALU.add, [r_ss[b]], [r_ss[b]])
            self.tt("pool", rstd[b], ss[b], self.neghalf, ALU.pow, [r_ss[b], self.r_nh], [r_ss[b]])
            self.stt(tmp[b], o, rstd[b], self.GG[cond], ALU.mult, ALU.mult, [pres[ob], pres[ob + 1], r_ss[b], self.r_mod], [r_tmp[b]])
            self.tt("dve", xn[b], tmp[b], xt[b], ALU.add, [r_tmp[b], r_xt[b]], [r_xn[b]])
            if l == DEPTH - 1:
                if T < 32:
                    dst = self.out["ys"][T * 128:(T + 1) * 128, :]
                else:
                    dst = self.out["yp"][(T - 32) * 128:(T - 31) * 128, :]
            else:
                dst = S["xcur"][T * 128:(T + 1) * 128, :]
            self.dma(STQ, dst, xn[b], reads=[r_xn[b]])
        A.release()


_CACHE = {}


def _get_program(debug=(), nlayers=DEPTH, phases=None):
    key = (tuple(debug), nlayers, phases)
    if key not in _CACHE:
        _CACHE[key] = Builder(debug, nlayers, phases).build()
    return _CACHE[key]


def _host_inputs(inputs):
    f = lambda a: np.ascontiguousarray(np.asarray(a, dtype=np.float32))
    perm = _win_perm()
    w_in = f(inputs["w_in"])[:, :, perm]
    consts = _consts()
    shared = {
        "w_ada": f(inputs["w_ada"]), "b_ada": f(inputs["b_ada"]),
        "norm_pre": f(inputs["norm_pre"]), "norm_post": f(inputs["norm_post"]),
        "w_in": np.ascontiguousarray(w_in), "w_out": f(inputs["w_out"]),
    }
    shared.update(consts)
    shared["attn_sink"] = f(inputs["attn_sink"])
    shared["diff_lambda"] = f(inputs["diff_lambda"]).reshape(DEPTH, 128)
    shared["diff_subln"] = f(inputs["diff_subln"])
    cw = f(inputs["hy_conv_w"])
    cb = f(inputs["hy_conv_b"])
    hcw = np.concatenate([cw, cb[:, None, :]], axis=1)
    shared["hcw"] = np.ascontiguousarray(hcw.reshape(DEPTH, 4, 6, 128).transpose(0, 3, 2, 1))
    shared["hskip"] = np.ascontiguousarray(f(inputs["hy_skip"]).reshape(DEPTH, 2, 128).transpose(0, 2, 1))
    shared["hw1"] = f(inputs["hy_filt_w1"])
    shared["hb1"] = f(inputs["hy_filt_b1"]).reshape(DEPTH, 64, 1)
    shared["hw2"] = f(inputs["hy_filt_w2"])
    shared["hb2"] = f(inputs["hy_filt_b2"]).reshape(DEPTH, 64, 1)
    shared["hw3"] = f(inputs["hy_filt_w3"])
    shared["hfreq"] = f(inputs["hy_filt_freq"]).reshape(DEPTH, 64, 1)
    shared["fn_w"] = f(inputs["fn_w"])
    shared["fn_b"] = f(inputs["fn_b"])
    xs = f(inputs["x_sample"])
    xp = f(inputs["x_prompt"])
    c = f(inputs["c"])
    cctx = f(inputs["c_ctx"])
    cak = f(inputs["cache_attn_k"]); cav = f(inputs["cache_attn_v"])
    cdk = f(inputs["cache_diff_k"]); cdv = f(inputs["cache_diff_v"])
    maps = []
    rev_over = {}
    for k_ in ("fcs", "fss", "fcp", "fsp"):
        rev_over[k_] = shared.pop(k_ + "r")
    rev_over["rope"] = np.ascontiguousarray(shared["rope"][::-1])
    hw3 = shared["hw3"]
    rev_over["hw3"] = np.ascontiguousarray(np.concatenate([hw3[:, :, 256:], hw3[:, :, :256]], axis=2))
    rev_over["hcw"] = np.ascontiguousarray(shared["hcw"][:, :, :, [2, 1, 0, 3]])
    rev_over["hflag"] = np.array([[0.0, 1.0]], np.float32)
    shared["hflag"] = np.array([[1.0, 0.0]], np.float32)
    for core in range(NCORES):
        b = core // 2
        m = dict(shared)
        if core % 2 == 1:
            m.update(rev_over)
            m["xs"] = np.ascontiguousarray(xs[b][::-1])
            m["xp"] = np.ascontiguousarray(xp[4 * core:4 * core + 4][:, ::-1].reshape(NPR * LP, D))
        else:
            m["xs"] = xs[b]
            m["xp"] = np.ascontiguousarray(xp[4 * core:4 * core + 4].reshape(NPR * LP, D))
        cond = np.stack([c[b], cctx], axis=-1)
        m["condT"] = np.ascontiguousarray(cond.reshape(8, 128, 2).transpose(1, 0, 2))
        m["cak"] = np.ascontiguousarray(cak[b].reshape(DEPTH, PAST, 128))
        m["cav"] = np.ascontiguousarray(cav[b].reshape(DEPTH, PAST, 128))
        m["cdk"] = np.ascontiguousarray(cdk[b].reshape(DEPTH, PAST, 256))
        m["cdv"] = np.ascontiguousarray(cdv[b].reshape(DEPTH, PAST, 256))
        maps.append(m)
    return maps


def _run(inputs, debug=(), nlayers=DEPTH, phases=None, cores=None, trace=False):
    nc = _get_program(debug, nlayers, phases)
    maps = _host_inputs(inputs)
    if cores is not None:
        maps = [maps[c] for c in cores]
    if trace:
        res = run_bass_kernel_spmd(nc, maps, core_ids=list(range(len(maps))), trace=True)
        return res
    res = run_bass_kernel_spmd(nc, maps, core_ids=list(range(len(maps))))
    return res.results


def kernel(**inputs):
    r = _run(inputs)

    def fix(i, a, axis):
        return np.flip(a, axis=axis) if i % 2 == 1 else a

    yp = np.concatenate([fix(i, r[i]["yp"].reshape(NPR, LP, D), 1) for i in range(NCORES)], axis=0)
    ys = np.stack([np.concatenate([r[2 * b]["ys"], r[2 * b + 1]["ys"][::-1]], axis=0) for b in range(4)], axis=0)
    nak = np.concatenate([fix(i, r[i]["nak"].reshape(NPR, DEPTH, LP, 2, 64), 2) for i in range(NCORES)], axis=0)
    nav = np.concatenate([fix(i, r[i]["nav"].reshape(NPR, DEPTH, LP, 2, 64), 2) for i in range(NCORES)], axis=0)
    ndk = np.concatenate([fix(i, r[i]["ndk"].reshape(NPR, DEPTH, LP, 2, 4, 32), 2) for i in range(NCORES)], axis=0)
    ndv = np.concatenate([fix(i, r[i]["ndv"].reshape(NPR, DEPTH, LP, 4, 64), 2) for i in range(NCORES)], axis=0)
    return (yp.astype(np.float32), ys.astype(np.float32), nak.astype(np.float32), nav.astype(np.float32),
            ndk.astype(np.float32), ndv.astype(np.float32))
```
